# Optimizing a Trainium2 kernel written in Bass

```python
import math
import numpy as np
import jax
import jax.numpy as jnp
from jax import lax

D_MODEL = 2048
BATCH = 4
SEQ = 2048
DEPTH = 2

HEAD_DIM = 128
ROPE_THETA = 500000.0
ROPE_FRACTION = 4
Q_BLOCK = 128
EPS = 1e-6
NEG_INF = -1e30

DA_HEADS = 4
DA_QK_DIM = 64
DA_V_DIM = 2 * DA_QK_DIM

NSA_HEADS = 8
NSA_GROUPS = 2
NSA_HPG = NSA_HEADS // NSA_GROUPS
NSA_BLOCK = 64
NSA_TOP_N = 8
NSA_WINDOW = 512
NSA_FORCED_SCORE = 1e4

FOX_HEADS = 4
FOX_BIAS_INIT = 3.0

D_FF = 256 * ((8 * D_MODEL // 3 + 255) // 256)

DA_QK_COLS = DA_HEADS * 2 * DA_QK_DIM
DA_V_COLS = DA_HEADS * DA_V_DIM
NSA_Q_COLS = NSA_HEADS * HEAD_DIM
NSA_KV_COLS = 3 * 2 * NSA_GROUPS * HEAD_DIM
NSA_GATE_COLS = 3 * NSA_HEADS
FOX_COLS = FOX_HEADS * HEAD_DIM
MERGE_COLS = 3 * D_MODEL
IN_COLS = 2 * DA_QK_COLS + DA_V_COLS + NSA_Q_COLS + NSA_KV_COLS + NSA_GATE_COLS + 3 * FOX_COLS + FOX_HEADS + MERGE_COLS

kernel_name = 'hybrid_diff_nsa_fox_macaron'


def rms_norm(x, gain):
    xf = x.astype(jnp.float32)
    y = xf * lax.rsqrt(jnp.mean(xf * xf, axis=-1, keepdims=True) + EPS)
    return (y * gain.astype(jnp.float32)).astype(x.dtype)


def rope_cos_sin(positions, rot_dim):
    inv_freq = ROPE_THETA ** (-jnp.arange(0, rot_dim, 2, dtype=jnp.float32) / rot_dim)
    ang = positions.astype(jnp.float32)[..., None] * inv_freq
    return jnp.cos(ang)[:, :, None, :], jnp.sin(ang)[:, :, None, :]


def partial_rope(x, cos, sin):
    half = cos.shape[-1]
    xf = x.astype(jnp.float32)
    x1 = xf[..., :half]
    x2 = xf[..., half:2 * half]
    out = jnp.concatenate([x1 * cos - x2 * sin, x2 * cos + x1 * sin, xf[..., 2 * half:]], axis=-1)
    return out.astype(x.dtype)


def masked_softmax(s, mask):
    return jax.nn.softmax(jnp.where(mask, s.astype(jnp.float32), NEG_INF), axis=-1)


def causal_mask(lo, n, hi):
    return (lo + jnp.arange(n))[:, None] >= jnp.arange(hi)[None, :]


def swiglu(h, w_gate, w_up, w_down):
    return (jax.nn.silu(h @ w_gate) * (h @ w_up)) @ w_down


def diff_attention(q, k, v, lam, out_gain, lam_init):
    B, S, H, _, dq = q.shape
    scale = dq ** -0.5
    outs = []
    for c in range(S // Q_BLOCK):
        lo, hi = c * Q_BLOCK, (c + 1) * Q_BLOCK
        s = jnp.einsum('bqhmd,bkhmd->bhmqk', q[:, lo:hi], k[:, :hi]).astype(jnp.float32) * scale
        p = masked_softmax(s, causal_mask(lo, Q_BLOCK, hi))
        w = p[:, :, 0] - lam * p[:, :, 1]
        outs.append(jnp.einsum('bhqk,bkhd->bqhd', w.astype(v.dtype), v[:, :hi]))
    o = jnp.concatenate(outs, axis=1)
    o = rms_norm(o, out_gain) * (1.0 - lam_init)
    return o.reshape(B, S, H * o.shape[-1])


def forgetting_attention(q, k, v, log_f):
    B, S, H, d = q.shape
    scale = d ** -0.5
    cum = jnp.cumsum(log_f, axis=1).transpose(0, 2, 1)
    outs = []
    for c in range(S // Q_BLOCK):
        lo, hi = c * Q_BLOCK, (c + 1) * Q_BLOCK
        s = jnp.einsum('bqhd,bkhd->bhqk', q[:, lo:hi], k[:, :hi]).astype(jnp.float32) * scale
        s = s + cum[:, :, lo:hi, None] - cum[:, :, None, :hi]
        p = masked_softmax(s, causal_mask(lo, Q_BLOCK, hi))
        outs.append(jnp.einsum('bhqk,bkhd->bqhd', p.astype(v.dtype), v[:, :hi]))
    return jnp.concatenate(outs, axis=1).reshape(B, S, H * d)


def nsa_attention(q, k_c, v_c, k_s, v_s, k_w, v_w, gates, cmp_pos, cmp_w1, cmp_w2, k_cmp_gain):
    B, S, G, Hg, d = q.shape
    L = NSA_BLOCK
    NB = S // L
    W = NSA_WINDOW
    n_sel = min(NSA_TOP_N, NB)
    scale = d ** -0.5
    t = jnp.arange(S)

    def compress(tok, pos, w1, w2):
        blk = tok.reshape(B, NB, L, G, d) + pos[None, None, :, None, :]
        blk = blk.transpose(0, 1, 3, 2, 4).reshape(B, NB, G, L * d)
        return jax.nn.gelu(blk @ w1) @ w2

    kc = rms_norm(compress(k_c, cmp_pos[0], cmp_w1[0], cmp_w2[0]), k_cmp_gain)
    vc = compress(v_c, cmp_pos[1], cmp_w1[1], cmp_w2[1])
    blk_end = jnp.arange(NB) * L + L - 1
    cmp_mask = blk_end[None, :] <= t[:, None]
    s_c = jnp.einsum('bsghd,bngd->bsghn', q, kc).astype(jnp.float32) * scale
    p_c = masked_softmax(s_c, cmp_mask[:, None, None, :]) * cmp_mask[:, None, None, :]
    o_cmp = jnp.einsum('bsghn,bngd->bsghd', p_c.astype(vc.dtype), vc)

    importance = p_c.sum(axis=3)
    blk = jnp.arange(NB)[None, :]
    cur = (t // L)[:, None]
    valid = blk <= cur
    forced = (blk == 0) | (blk == cur) | (blk == cur - 1)
    score = jnp.where(forced[:, None, :], NSA_FORCED_SCORE,
                      jnp.where(valid[:, None, :], importance, -1.0))
    _, sel = lax.top_k(score, n_sel)

    ks_blk = k_s.reshape(B, NB, L, G, d).transpose(0, 3, 1, 2, 4)
    vs_blk = v_s.reshape(B, NB, L, G, d).transpose(0, 3, 1, 2, 4)
    gather = jax.vmap(jax.vmap(lambda table, i: table[i]))
    kw_pad = jnp.pad(k_w, ((0, 0), (W, 0), (0, 0), (0, 0)))
    vw_pad = jnp.pad(v_w, ((0, 0), (W, 0), (0, 0), (0, 0)))

    def chunk(c):
        lo = c * Q_BLOCK
        tq = lo + jnp.arange(Q_BLOCK)
        qc = lax.dynamic_slice_in_dim(q, lo, Q_BLOCK, axis=1)
        idx = lax.dynamic_slice_in_dim(sel, lo, Q_BLOCK, axis=1).transpose(0, 2, 1, 3)
        kg = gather(ks_blk, idx)
        vg = gather(vs_blk, idx)
        s_s = jnp.einsum('bqghd,bgqnld->bqghnl', qc, kg).astype(jnp.float32) * scale
        tok = idx[..., None] * L + jnp.arange(L)
        m_s = (tok <= tq[None, None, :, None, None]).transpose(0, 2, 1, 3, 4)
        p_s = masked_softmax(s_s.reshape(B, Q_BLOCK, G, Hg, n_sel * L),
                             m_s.reshape(B, Q_BLOCK, G, 1, n_sel * L))
        p_s = p_s.reshape(B, Q_BLOCK, G, Hg, n_sel, L)
        o_s = jnp.einsum('bqghnl,bgqnld->bqghd', p_s.astype(vg.dtype), vg)
        kwc = lax.dynamic_slice_in_dim(kw_pad, lo, W + Q_BLOCK, axis=1)
        vwc = lax.dynamic_slice_in_dim(vw_pad, lo, W + Q_BLOCK, axis=1)
        kpos = lo - W + jnp.arange(W + Q_BLOCK)
        m_w = ((kpos[None, :] <= tq[:, None]) & (kpos[None, :] > tq[:, None] - W)
               & (kpos[None, :] >= 0))
        s_w = jnp.einsum('bqghd,bkgd->bqghk', qc, kwc).astype(jnp.float32) * scale
        p_w = masked_softmax(s_w, m_w[:, None, None, :])
        o_w = jnp.einsum('bqghk,bkgd->bqghd', p_w.astype(vwc.dtype), vwc)
        return o_s, o_w

    o_s, o_w = lax.map(chunk, jnp.arange(S // Q_BLOCK))
    o_s = jnp.moveaxis(o_s, 0, 1).reshape(B, S, G, Hg, d)
    o_w = jnp.moveaxis(o_w, 0, 1).reshape(B, S, G, Hg, d)
    o = gates[..., 0:1] * o_cmp + gates[..., 1:2] * o_s + gates[..., 2:3] * o_w
    return o.reshape(B, S, G * Hg * d)


def token_mixer(h, cos_a, sin_a, cos_b, sin_b, lam_init, w_in, da_q_norm, da_k_norm, da_lambda,
                da_out_norm, nsa_q_norm, nsa_k_norm, nsa_cmp_pos, nsa_cmp_w1, nsa_cmp_w2,
                fox_q_norm, fox_k_norm, fox_f_bias, w_branch_a, w_branch_b, w_branch_c, w_out):
    B, S, _ = h.shape
    widths = [DA_QK_COLS, DA_QK_COLS, DA_V_COLS, NSA_Q_COLS, NSA_KV_COLS, NSA_GATE_COLS,
              FOX_COLS, FOX_COLS, FOX_COLS, FOX_HEADS, MERGE_COLS]
    (a_q, a_k, a_v, b_q, b_kv, b_g, c_q, c_k, c_v, c_f, g_m) = jnp.split(
        h @ w_in, np.cumsum(widths)[:-1].tolist(), axis=-1)

    qa = partial_rope(rms_norm(a_q.reshape(B, S, 2 * DA_HEADS, DA_QK_DIM), da_q_norm), cos_a, sin_a)
    ka = partial_rope(rms_norm(a_k.reshape(B, S, 2 * DA_HEADS, DA_QK_DIM), da_k_norm), cos_a, sin_a)
    qa = qa.reshape(B, S, DA_HEADS, 2, DA_QK_DIM)
    ka = ka.reshape(B, S, DA_HEADS, 2, DA_QK_DIM)
    va = a_v.reshape(B, S, DA_HEADS, DA_V_DIM)
    lp = da_lambda.astype(jnp.float32)
    lam = jnp.exp(jnp.sum(lp[0] * lp[1])) - jnp.exp(jnp.sum(lp[2] * lp[3])) + lam_init
    o_a = diff_attention(qa, ka, va, lam, da_out_norm, lam_init)

    qb = partial_rope(rms_norm(b_q.reshape(B, S, NSA_HEADS, HEAD_DIM), nsa_q_norm), cos_b, sin_b)
    qb = qb.reshape(B, S, NSA_GROUPS, NSA_HPG, HEAD_DIM)
    kv = b_kv.reshape(B, S, 3, 2, NSA_GROUPS, HEAD_DIM)
    k_cmp = partial_rope(kv[:, :, 0, 0], cos_b, sin_b)
    k_slc = partial_rope(rms_norm(kv[:, :, 1, 0], nsa_k_norm[1]), cos_b, sin_b)
    k_win = partial_rope(rms_norm(kv[:, :, 2, 0], nsa_k_norm[2]), cos_b, sin_b)
    gates = jax.nn.sigmoid(b_g.reshape(B, S, NSA_GROUPS, NSA_HPG, 3))
    o_b = nsa_attention(qb, k_cmp, kv[:, :, 0, 1], k_slc, kv[:, :, 1, 1], k_win, kv[:, :, 2, 1],
                        gates, nsa_cmp_pos, nsa_cmp_w1, nsa_cmp_w2, nsa_k_norm[0])

    qc = rms_norm(c_q.reshape(B, S, FOX_HEADS, HEAD_DIM), fox_q_norm)
    kc = rms_norm(c_k.reshape(B, S, FOX_HEADS, HEAD_DIM), fox_k_norm)
    vc = c_v.reshape(B, S, FOX_HEADS, HEAD_DIM)
    log_f = jax.nn.log_sigmoid(c_f.astype(jnp.float32) + fox_f_bias.astype(jnp.float32))
    o_c = forgetting_attention(qc, kc, vc, log_f)

    gm = jax.nn.sigmoid(g_m).reshape(B, S, 3, D_MODEL)
    y = (gm[:, :, 0] * (o_a @ w_branch_a) + gm[:, :, 1] * (o_b @ w_branch_b)
         + gm[:, :, 2] * (o_c @ w_branch_c))
    return y @ w_out


def setup_inputs(seed: int = 0) -> dict:
    key = jax.random.key(seed)
    keys = iter(jax.random.split(key, 40))

    def nrm(shape, scale):
        return jax.random.normal(next(keys), shape, jnp.float32) * scale

    def gain(shape):
        return 1.0 + nrm(shape, 0.02)

    D, F, L, d = D_MODEL, D_FF, NSA_BLOCK, HEAD_DIM
    x = jax.random.normal(next(keys), (BATCH, SEQ, D), jnp.float32)
    offsets = jax.random.randint(next(keys), (BATCH, 1), 0, 1024)
    positions = (jnp.arange(SEQ)[None, :] + offsets).astype(jnp.int32)
    return {
        'x': x,
        'positions': positions,
        'ffn1_norm': gain((DEPTH, D)),
        'ffn1_w_gate': nrm((DEPTH, D, F), D ** -0.5),
        'ffn1_w_up': nrm((DEPTH, D, F), D ** -0.5),
        'ffn1_w_down': nrm((DEPTH, F, D), F ** -0.5),
        'mix_norm': gain((DEPTH, D)),
        'w_in': nrm((DEPTH, D, IN_COLS), D ** -0.5),
        'da_q_norm': gain((DEPTH, DA_QK_DIM)),
        'da_k_norm': gain((DEPTH, DA_QK_DIM)),
        'da_lambda': nrm((DEPTH, 4, DA_QK_DIM), 0.1),
        'da_out_norm': gain((DEPTH, DA_V_DIM)),
        'nsa_q_norm': gain((DEPTH, d)),
        'nsa_k_norm': gain((DEPTH, 3, d)),
        'nsa_cmp_pos': nrm((DEPTH, 2, L, d), 0.02),
        'nsa_cmp_w1': nrm((DEPTH, 2, L * d, d), (L * d) ** -0.5),
        'nsa_cmp_w2': nrm((DEPTH, 2, d, d), d ** -0.5),
        'fox_q_norm': gain((DEPTH, d)),
        'fox_k_norm': gain((DEPTH, d)),
        'fox_f_bias': FOX_BIAS_INIT + nrm((DEPTH, FOX_HEADS), 0.5),
        'w_branch_a': nrm((DEPTH, DA_V_COLS, D), DA_V_COLS ** -0.5),
        'w_branch_b': nrm((DEPTH, NSA_Q_COLS, D), NSA_Q_COLS ** -0.5),
        'w_branch_c': nrm((DEPTH, FOX_COLS, D), FOX_COLS ** -0.5),
        'w_out': nrm((DEPTH, D, D), D ** -0.5),
        'ffn2_norm': gain((DEPTH, D)),
        'ffn2_w_gate': nrm((DEPTH, D, F), D ** -0.5),
        'ffn2_w_up': nrm((DEPTH, D, F), D ** -0.5),
        'ffn2_w_down': nrm((DEPTH, F, D), F ** -0.5),
    }


def reference(x, positions, ffn1_norm, ffn1_w_gate, ffn1_w_up, ffn1_w_down, mix_norm, w_in,
              da_q_norm, da_k_norm, da_lambda, da_out_norm, nsa_q_norm, nsa_k_norm, nsa_cmp_pos,
              nsa_cmp_w1, nsa_cmp_w2, fox_q_norm, fox_k_norm, fox_f_bias, w_branch_a, w_branch_b,
              w_branch_c, w_out, ffn2_norm, ffn2_w_gate, ffn2_w_up, ffn2_w_down):
    cos_a, sin_a = rope_cos_sin(positions, DA_QK_DIM // ROPE_FRACTION)
    cos_b, sin_b = rope_cos_sin(positions, HEAD_DIM // ROPE_FRACTION)
    for l in range(DEPTH):
        lam_init = 0.8 - 0.6 * math.exp(-0.3 * l)
        x = x + 0.5 * swiglu(rms_norm(x, ffn1_norm[l]), ffn1_w_gate[l], ffn1_w_up[l], ffn1_w_down[l])
        x = x + token_mixer(rms_norm(x, mix_norm[l]), cos_a, sin_a, cos_b, sin_b, lam_init, w_in[l],
                            da_q_norm[l], da_k_norm[l], da_lambda[l], da_out_norm[l],
                            nsa_q_norm[l], nsa_k_norm[l], nsa_cmp_pos[l], nsa_cmp_w1[l], nsa_cmp_w2[l],
                            fox_q_norm[l], fox_k_norm[l], fox_f_bias[l],
                            w_branch_a[l], w_branch_b[l], w_branch_c[l], w_out[l])
        x = x + 0.5 * swiglu(rms_norm(x, ffn2_norm[l]), ffn2_w_gate[l], ffn2_w_up[l], ffn2_w_down[l])
    return x
```

```python
import contextlib
import math

import numpy as np

import concourse.bass as bass
import concourse.mybir as mybir
from concourse.bass_utils import run_bass_kernel_spmd

F32 = mybir.dt.float32
BF16 = mybir.dt.bfloat16
I32 = mybir.dt.int32
AF = mybir.ActivationFunctionType
ALU = mybir.AluOpType
AX = mybir.AxisListType

D = 2048
SEQ = 2048
DEPTH = 2
FF = 5632
NFC = FF // 128
CH = 1024
NCHUNK = SEQ // CH
TPC = CH // 128
NT = SEQ // 128
EPS = 1e-6
IN_COLS = 11804
GM0 = 5660
COL = dict(aq=0, ak=512, av=1024, bq=1536, bkv=2560, bg=4096, cq=4120, ck=4632, cv=5144, cf=5656)
FGROUPS = [12, 12, 12, 8]
N_ACTIVE = 4
DBG = {"stop": None}


class _Stop(Exception):
    pass


class Op:
    __slots__ = ("eng", "fn", "reads", "writes", "dma", "deps", "signal", "semi", "semval", "idx")


class Sched:
    ENGS = ["pe", "act", "dve", "pool", "sp"]
    NDS = 12

    def __init__(self):
        self.ops = []
        self.collect = False

    def op(self, eng, fn, reads=(), writes=(), dma=False):
        if self.collect:
            return
        o = Op()
        o.eng, o.fn, o.reads, o.writes, o.dma = eng, fn, tuple(reads), tuple(writes), dma
        o.deps = None
        o.signal = False
        o.idx = len(self.ops)
        self.ops.append(o)

    def barrier(self):
        if self.collect:
            return
        o = Op()
        o.eng, o.fn, o.reads, o.writes, o.dma = "barrier", None, (), (), False
        o.idx = len(self.ops)
        self.ops.append(o)

    def analyse(self):
        writers = {}
        readers = {}
        ops = self.ops
        last_on = {}
        dmas_since = []
        pending = {e: set() for e in self.ENGS}
        for o in ops:
            if o.eng == "barrier":
                deps = set(last_on.values()) | set(dmas_since)
                for e in self.ENGS:
                    pending[e] |= deps
                dmas_since = []
                continue
            deps = set()
            if pending[o.eng]:
                deps |= pending[o.eng]
                pending[o.eng] = set()
            for k in o.reads:
                for w in writers.get(k, ()):
                    deps.add(w)
            for k in o.writes:
                for w in writers.get(k, ()):
                    if not (o.dma and ops[w].dma):
                        deps.add(w)
                r = readers.get(k)
                if r is not None:
                    deps |= set(r[0].values())
                    deps |= set(r[1])
            for k in o.reads:
                r = readers.setdefault(k, ({}, []))
                if o.dma:
                    r[1].append(o.idx)
                else:
                    r[0][o.eng] = o.idx
            for k in o.writes:
                r = readers.get(k)
                if r is not None and (r[0] or r[1]):
                    writers[k] = [o.idx]
                    readers[k] = ({}, [])
                else:
                    writers.setdefault(k, []).append(o.idx)
            deps.discard(o.idx)
            need = []
            for d in deps:
                od = ops[d]
                if od.dma or o.dma or od.eng != o.eng or o.eng != "pe":
                    need.append(d)
                    od.signal = True
            o.deps = need
            if o.dma:
                o.signal = True
                dmas_since.append(o.idx)
            else:
                last_on[o.eng] = o.idx

    def emit(self, nc, st):
        self.analyse()
        sems = {e: st.enter_context(nc.semaphore(f"s_{e}")) for e in self.ENGS}
        dsems = {e: [st.enter_context(nc.semaphore(f"d_{e}{i}")) for i in range(self.NDS)] for e in ("sp", "pool", "act")}
        allsems = list(sems.values()) + [s for v in dsems.values() for s in v]
        cnt = {e: 0 for e in self.ENGS}
        dcnt = {e: 0 for e in dsems}
        dval = {e: [0] * self.NDS for e in dsems}
        for o in self.ops:
            if o.eng == "barrier":
                continue
            if o.dma:
                i = dcnt[o.eng] % self.NDS
                dcnt[o.eng] += 1
                dval[o.eng][i] += 16
                o.semi = dsems[o.eng][i]
                o.semval = dval[o.eng][i]
            elif o.signal:
                cnt[o.eng] += 1
                o.semi = sems[o.eng]
                o.semval = cnt[o.eng]
        engmap = {"pe": nc.tensor, "act": nc.scalar, "dve": nc.vector, "pool": nc.gpsimd, "sp": nc.sync}
        with nc.Block() as b0:
            @b0.gpsimd
            def _(g):
                for s in allsems:
                    g.sem_clear(s)
        ops = self.ops
        with nc.Block() as block:
            deco = {"pe": block.tensor, "act": block.scalar, "dve": block.vector, "pool": block.gpsimd, "sp": block.sync}
            for ename in self.ENGS:
                mine = [o for o in ops if o.eng == ename]

                def body(e, mine=mine, ename=ename):
                    seen = {}
                    for o in mine:
                        waits = {}
                        for d in o.deps:
                            od = ops[d]
                            key = id(od.semi)
                            if seen.get(key, 0) >= od.semval:
                                continue
                            if key not in waits or waits[key][1] < od.semval:
                                waits[key] = (od.semi, od.semval)
                        if o.dma and o.semval > 16:
                            key = id(o.semi)
                            if seen.get(key, 0) < o.semval - 16:
                                if key not in waits or waits[key][1] < o.semval - 16:
                                    waits[key] = (o.semi, o.semval - 16)
                        for key, (sm, v) in waits.items():
                            e.wait_ge(sm, v)
                            seen[key] = v
                        if o.fn is None:
                            continue
                        ins = o.fn(e)
                        if o.dma:
                            ins.then_inc(o.semi, 16)
                        elif o.signal:
                            ins.then_inc(o.semi, 1)

                deco[ename](body)


def _consts():
    c = {}
    p = np.arange(128)
    c["ident"] = np.eye(128, dtype=np.float32)
    c["tri"] = (p[:, None] <= p[None, :]).astype(np.float32)
    c["band"] = (p[:, None] > p[None, :]).astype(np.float32)
    t = np.arange(SEQ)
    blk = np.arange(32)
    cm = ((blk[:, None] + 1) * 64 - 1 <= t[None, :]).astype(np.float32)
    cmask = np.zeros((128, SEQ), np.float32)
    cmask[:32] = cm
    c["cmask"] = cmask
    cur = t // 64
    valid = blk[None, :] <= cur[:, None]
    forced = (blk[None, :] == 0) | (blk[None, :] == cur[:, None]) | (blk[None, :] == cur[:, None] - 1)
    A = (valid & ~forced).astype(np.float32)
    B = np.where(forced, 1.0e4, np.where(valid, 0.0, -1.0)).astype(np.float32)
    c["selA"] = A.reshape(NT, 128, 32).transpose(1, 0, 2).reshape(128, NT * 32).copy()
    c["selB"] = B.reshape(NT, 128, 32).transpose(1, 0, 2).reshape(128, NT * 32).copy()
    E = np.zeros((128, NT * 128), np.float32)
    for kt in range(NT):
        for s in range(128):
            E[2 * kt + s // 64, kt * 128 + s] = 1.0
    c["E"] = E
    invf = np.zeros((128, 24), np.float32)
    ia = (np.float32(500000.0) ** (-np.arange(0, 16, 2, dtype=np.float32) / np.float32(16))).astype(np.float32)
    ib = (np.float32(500000.0) ** (-np.arange(0, 32, 2, dtype=np.float32) / np.float32(32))).astype(np.float32)
    invf[:, :8] = ia[None]
    invf[:, 8:] = ib[None]
    c["invf"] = invf
    return c


CONST_F32 = ["ident", "tri", "selA", "selB", "invf"]
CONST_BF = ["ident", "tri", "band", "cmask", "E"]


def build_program():
    nc = bass.Bass("TRN2", target_bir_lowering=False)
    S = Sched()
    dram = {}

    def din(name, shape, dt=F32):
        dram[name] = nc.dram_tensor(name, list(shape), dt, kind="ExternalInput").ap()
        return dram[name]

    def dscr(name, shape, dt=BF16):
        if DBG["stop"]:
            dram[name] = nc.dram_tensor(name, list(shape), dt, kind="ExternalOutput").ap()
        else:
            dram[name] = nc.dram_tensor(name, list(shape), dt).ap()
        return dram[name]

    xT_d = din("xT", [D, SEQ])
    pos_d = din("pos", [128, NT], I32)
    outT_d = nc.dram_tensor("outT", [D, SEQ], F32, kind="ExternalOutput").ap()
    W = {}
    for nm, shp in [("ffn1_norm", [DEPTH, 128, 16]), ("ffn1_w_gate", [DEPTH, D, FF]), ("ffn1_w_up", [DEPTH, D, FF]),
                    ("ffn1_w_down", [DEPTH, FF, D]), ("mix_norm", [DEPTH, 128, 16]), ("w_in", [DEPTH, D, IN_COLS]),
                    ("da_q_norm", [DEPTH, 64]), ("da_k_norm", [DEPTH, 64]), ("da_lambda", [DEPTH, 256]),
                    ("da_out_norm", [DEPTH, 128]), ("nsa_q_norm", [DEPTH, 128]), ("nsa_k_norm", [DEPTH, 3 * 128]),
                    ("nsa_cmp_pos", [DEPTH, 2, 128, 64]), ("nsa_cmp_w1", [DEPTH, 2, 8192, 128]),
                    ("nsa_cmp_w2", [DEPTH, 2, 128, 128]), ("fox_q_norm", [DEPTH, 128]), ("fox_k_norm", [DEPTH, 128]),
                    ("fox_f_bias", [DEPTH, 4]), ("w_branch_a", [DEPTH, 512, D]), ("w_branch_b", [DEPTH, 1024, D]),
                    ("w_branch_c", [DEPTH, 512, D]), ("w_out", [DEPTH, D, D]), ("ffn2_norm", [DEPTH, 128, 16]),
                    ("ffn2_w_gate", [DEPTH, D, FF]), ("ffn2_w_up", [DEPTH, D, FF]), ("ffn2_w_down", [DEPTH, FF, D])]:
        W[nm] = din(nm, shp)
    consts = _consts()
    cd = {k: din("c_" + k, consts[k].shape) for k in consts}

    QT_d = dscr("QT", [2048, CH])
    KTa_d = [dscr(f"KTa{l}", [512, SEQ]) for l in range(DEPTH)]
    KTc_d = [dscr(f"KTc{l}", [512, SEQ]) for l in range(DEPTH)]
    KTs_d = [dscr(f"KTs{l}", [256, SEQ]) for l in range(DEPTH)]
    KTw_d = [dscr(f"KTw{l}", [256, SEQ]) for l in range(DEPTH)]
    Va_d = [dscr(f"Va{l}", [SEQ, 512]) for l in range(DEPTH)]
    Vc_d = [dscr(f"Vc{l}", [SEQ, 512]) for l in range(DEPTH)]
    Vs_d = [dscr(f"Vs{l}", [SEQ, 256]) for l in range(DEPTH)]
    Vw_d = [dscr(f"Vw{l}", [SEQ, 256]) for l in range(DEPTH)]
    kcT_d = [dscr(f"kcT{l}", [2, 128, 32]) for l in range(DEPTH)]
    vcB_d = [dscr(f"vcB{l}", [2, 32, 128]) for l in range(DEPTH)]
    gmT_d = dscr("gmT", [3 * D, CH])
    oT_d = dscr("oTs", [D, CH])
    dbgh_d = dscr("dbgh", [128, 16 * CH]) if DBG["stop"] else None
    dbgstg_d = dscr("dbgstg", [128, 4 * CH]) if DBG["stop"] else None
    dbgh1_d = dscr("dbgh1", [2, 128, 32], F32) if DBG["stop"] else None

    st = contextlib.ExitStack()
    with st:
        def sb(name, shape, dt=F32):
            return st.enter_context(nc.sbuf_tensor("s_" + name, list(shape), dt))

        xT = sb("xT", [128, 16, CH])
        hT = sb("hT", [128, 16, CH], BF16)
        NSLOT = 2
        wsl = [sb(f"wsl{i}", [128, 8192], BF16) for i in range(NSLOT)]
        UN = 27136
        un = sb("union", [128, UN], BF16)
        identb = sb("identb", [128, 128], BF16)
        trib = sb("trib", [128, 128], BF16)
        bandb = sb("bandb", [128, 128], BF16)
        cmaskb = sb("cmaskb", [128, SEQ], BF16)
        Eb = sb("Eb", [128, NT * 128], BF16)
        trif = sb("trif", [128, 128])
        onesf = sb("onesf", [128, 128])
        selA = sb("selA", [128, NT, 32], BF16)
        selB = sb("selB", [128, NT, 32], BF16)
        invf = sb("invf", [128, 24])
        epsT = sb("epsT", [128, 1])
        posi = sb("posi", [128, NT], I32)
        posf = sb("posf", [128, NT])
        cosT = sb("cosT", [128, NT, 24])
        sinT = sb("sinT", [128, NT, 24])
        gains = sb("gains", [128, DEPTH, 3, 16])
        hg = sb("hg", [128, 9, 128])
        fbias = sb("fbias", [128, DEPTH, 4])
        lamt = sb("lamt", [128, DEPTH, 4])
        posT = sb("posT", [128, DEPTH, 2, 64])
        gates = sb("gates", [128, TPC, 24])
        cum = sb("cum", [128, DEPTH, NT, 4])
        carry = sb("carry", [128, DEPTH, NT + 1, 4])
        smallf = [sb(f"smallf{i}", [128, 64]) for i in range(8)]
        rot = {"smallf": 0}

        psf = [st.enter_context(nc.psum_tensor(f"psf{i}", [128, 512], F32)) for i in range(6)]
        psb = [st.enter_context(nc.psum_tensor(f"psb{i}", [128, 1024], BF16)) for i in range(2)]
        prot = {"f": 0, "b": 0}

        def psum():
            i = prot["f"] % 6
            prot["f"] += 1
            return ("psf", i), psf[i]

        def psumb():
            i = prot["b"] % 2
            prot["b"] += 1
            return ("psb", i), psb[i]

        def small():
            i = rot["smallf"] % 8
            rot["smallf"] += 1
            return ("smallf", i), smallf[i]

        class Carver:
            def __init__(self):
                self.off = 0

            def reset(self):
                self.off = 0

            def take(self, nel, dt=BF16):
                n16 = nel * (2 if dt == F32 else 1)
                a = self.off
                self.off += n16
                assert self.off <= UN, ("union overflow", self.off)
                ap = un[:, a:a + n16]
                if dt == F32:
                    ap = ap.bitcast(F32)
                return ap

        carv = Carver()
        ukey = {"n": 0}

        def ukeynew(tag):
            ukey["n"] += 1
            return ("u", tag, ukey["n"])

        def act(out, in_, func, reads, writes, bias=None, scale=None):
            kw = {}
            if bias is not None:
                kw["bias"] = bias
            if scale is not None:
                kw["scale"] = scale
            S.op("act", lambda e: e.activation(out=out, in_=in_, func=func, **kw), reads, writes)

        def tt(eng, out, in0, in1, op, reads, writes):
            S.op(eng, lambda e: e.tensor_tensor(out=out, in0=in0, in1=in1, op=op), reads, writes)

        def ts(eng, out, in0, s1, op0, reads, writes, s2=None, op1=None):
            if op1 is None:
                S.op(eng, lambda e: e.tensor_scalar(out=out, in0=in0, scalar1=s1, scalar2=None, op0=op0), reads, writes)
            else:
                S.op(eng, lambda e: e.tensor_scalar(out=out, in0=in0, scalar1=s1, scalar2=s2, op0=op0, op1=op1), reads, writes)

        def stt(out, in0, scalar, in1, op0, op1, reads, writes):
            S.op("dve", lambda e: e.scalar_tensor_tensor(out=out, in0=in0, scalar=scalar, in1=in1, op0=op0, op1=op1), reads, writes)

        def cp(eng, out, in_, reads, writes):
            if eng == "act":
                S.op("act", lambda e: e.activation(out=out, in_=in_, func=AF.Copy), reads, writes)
            else:
                S.op(eng, lambda e: e.tensor_copy(out=out, in_=in_), reads, writes)

        def recip(out, in_, reads, writes):
            S.op("dve", lambda e: e.reciprocal(out=out, in_=in_), reads, writes)

        def rsum(out, in_, reads, writes):
            S.op("dve", lambda e: e.tensor_reduce(out=out, in_=in_, axis=AX.X, op=ALU.add), reads, writes)

        def mm(out, pairs, reads, writes):
            def fn(e):
                ins = None
                n = len(pairs)
                for i, (l, r) in enumerate(pairs):
                    ins = e.matmul(out, lhsT=l, rhs=r, start=(i == 0), stop=(i == n - 1))
                return ins
            S.op("pe", fn, reads, writes)

        def tr(out, in_, ident, reads, writes):
            S.op("pe", lambda e: e.transpose(out, in_, ident), reads, writes)

        def dma(eng, out, in_, reads, writes):
            S.op(eng, lambda e: e.dma_start(out=out, in_=in_), reads, writes, dma=True)

        dbgd = {}

        def dbg_dump(name, ap, reads, shape, dt=F32):
            if not DBG["stop"]:
                return
            if name not in dbgd:
                dbgd[name] = nc.dram_tensor("dbg_" + name, list(shape), dt, kind="ExternalOutput").ap()
            if S.collect:
                return
            dma("sp", dbgd[name], ap, reads, [("dbgd", name)])
            DBG.setdefault("keys", []).append(("dbgd", name))

        def dbg_psum(name, psap, pk, n):
            if not DBG["stop"]:
                return
            k_, sm_ = small()
            cp("act", sm_[:, 0:n], psap, [pk], [k_])
            dbg_dump(name, sm_[:, 0:n], [k_], [128, n])

        def memset(eng, ap, val, writes):
            S.op(eng, lambda e: e.memset(ap, val), (), writes)

        wstate = {"specs": [], "n": 0, "issued": 0}

        def w_issue(upto):
            while wstate["issued"] <= upto and wstate["issued"] < len(wstate["specs"]):
                i = wstate["issued"]
                slot = i % NSLOT
                for (off, shape3, src) in wstate["specs"][i]:
                    k, c = shape3
                    dst = wsl[slot][:, off:off + k * c].rearrange("p (k c) -> p k c", c=c)
                    dma("pool", dst, src, (), [("w", slot)])
                wstate["issued"] += 1

        def wnext(spec):
            if S.collect:
                wstate["specs"].append(spec)
                return wsl[0], ("w", 0)
            n = wstate["n"]
            wstate["n"] += 1
            w_issue(n + NSLOT - 1)
            slot = n % NSLOT
            return wsl[slot], ("w", slot)

        def wview(slot_t, off, k, c):
            return slot_t[:, off:off + k * c].rearrange("p (k c) -> p k c", c=c)

        def wsrc(w2d, r0, nk, c0, ncol):
            return w2d[r0:r0 + nk * 128, c0:c0 + ncol].rearrange("(k p) c -> p k c", p=128)

        def setup():
            carv.reset()
            ang = carv.take(NT * 24, F32).rearrange("p (t f) -> p t f", f=24)
            kq = carv.take(NT * 24, F32).rearrange("p (t f) -> p t f", f=24)
            kqi = carv.take(NT * 24, F32).bitcast(I32).rearrange("p (t f) -> p t f", f=24)
            rr = carv.take(NT * 24, F32).rearrange("p (t f) -> p t f", f=24)
            msk = carv.take(NT * 24, F32).rearrange("p (t f) -> p t f", f=24)
            lamraw = carv.take(DEPTH * 256, F32).rearrange("p (l f) -> p l f", f=256)
            lamp = carv.take(128, F32)
            dma("pool", identb[:], cd["ident"][:, :], (), [("c", "ident")])
            for nm, t_ in [("tri", trib), ("band", bandb), ("cmask", cmaskb), ("E", Eb)]:
                dma("pool", t_[:], cd[nm][:, :], (), [("c", "masks")])
            dma("sp", trif[:], cd["tri"][:, :], (), [("c", "trif")])
            dma("pool", selA[:], cd["selA"][:, :].rearrange("p (t b) -> p t b", b=32), (), [("c", "selA")])
            dma("pool", selB[:], cd["selB"][:, :].rearrange("p (t b) -> p t b", b=32), (), [("c", "selB")])
            dma("sp", invf[:], cd["invf"][:, :], (), [("c", "invf")])
            dma("sp", posi[:], pos_d[:, :], (), [("c", "posi")])
            memset("dve", onesf[:], 1.0, [("c", "onesf")])
            memset("dve", epsT[:], EPS, [("c", "eps")])
            for l in range(DEPTH):
                for j, nm in enumerate(["ffn1_norm", "mix_norm", "ffn2_norm"]):
                    dma("sp", gains[:, l, j, :], W[nm][l, :, :], (), [("c", "gains")])
                dma("sp", fbias[:, l, :], W["fox_f_bias"][l:l + 1, :].partition_broadcast(128), (), [("c", "fbias")])
                dma("sp", lamraw[:, l, :], W["da_lambda"][l:l + 1, :].partition_broadcast(128), (), [("c", "lamraw")])
                for kv in range(2):
                    dma("sp", posT[:, l, kv, :], W["nsa_cmp_pos"][l, kv, :, :], (), [("c", "posT")])
            for l in range(DEPTH):
                lam_init = 0.8 - 0.6 * math.exp(-0.3 * l)
                tt("dve", lamp[:, 0:64], lamraw[:, l, 0:64], lamraw[:, l, 64:128], ALU.mult, [("c", "lamraw")], [("c", "lamp")])
                tt("dve", lamp[:, 64:128], lamraw[:, l, 128:192], lamraw[:, l, 192:256], ALU.mult, [("c", "lamraw")], [("c", "lamp")])
                rsum(lamt[:, l, 2:4], lamp[:].rearrange("p (a b) -> p a b", b=64), [("c", "lamp")], [("c", "lamt")])
                act(lamt[:, l, 2:4], lamt[:, l, 2:4], AF.Exp, [("c", "lamt")], [("c", "lamt")])
                tt("dve", lamt[:, l, 0:1], lamt[:, l, 2:3], lamt[:, l, 3:4], ALU.subtract, [("c", "lamt")], [("c", "lamt")])
                ts("dve", lamt[:, l, 0:1], lamt[:, l, 0:1], lam_init, ALU.add, [("c", "lamt")], [("c", "lamt")])
                ts("dve", lamt[:, l, 1:2], lamt[:, l, 0:1], -1.0, ALU.mult, [("c", "lamt")], [("c", "lamt")])
            R = [("c", "rope")]
            cp("dve", posf[:], posi[:], [("c", "posi")], R)
            tt("dve", ang, posf[:].unsqueeze(2).to_broadcast([128, NT, 24]),
               invf[:].unsqueeze(1).to_broadcast([128, NT, 24]), ALU.mult, R + [("c", "invf")], R)
            C1 = 6.28125
            C2 = 2.0 * math.pi - C1
            for which, outt in ((0, sinT), (1, cosT)):
                src = ang
                if which == 1:
                    ts("dve", msk, ang, math.pi / 2, ALU.add, R, R)
                    cp("dve", ang, msk, R, R)
                ts("dve", kq, src, 1.0 / (2 * math.pi), ALU.mult, R, R, s2=0.5, op1=ALU.add)
                cp("dve", kqi, kq, R, R)
                cp("dve", kq, kqi, R, R)
                stt(rr, kq, -C1, src, ALU.mult, ALU.add, R, R)
                stt(rr, kq, -C2, rr, ALU.mult, ALU.add, R, R)
                ts("dve", msk, rr, math.pi, ALU.is_gt, R, R, s2=-2.0 * math.pi, op1=ALU.mult)
                tt("dve", rr, rr, msk, ALU.add, R, R)
                ts("dve", msk, rr, -math.pi, ALU.is_lt, R, R, s2=2.0 * math.pi, op1=ALU.mult)
                tt("dve", rr, rr, msk, ALU.add, R, R)
                ts("dve", rr, rr, 3.1415925, ALU.min, R, R, s2=-3.1415925, op1=ALU.max)
                act(outt[:], rr, AF.Sin, R, R)
            for l in range(DEPTH):
                memset("dve", carry[:, l, 0, :], 0.0, [("carry", l)])

        def load_layer(l):
            lam_init = 0.8 - 0.6 * math.exp(-0.3 * l)
            for j, (nm, off, n) in enumerate([("da_q_norm", 0, 64), ("da_k_norm", 0, 64), ("nsa_q_norm", 0, 128),
                                              ("nsa_k_norm", 0, 128), ("nsa_k_norm", 128, 128), ("nsa_k_norm", 256, 128),
                                              ("fox_q_norm", 0, 128), ("fox_k_norm", 0, 128), ("da_out_norm", 0, 128)]):
                dma("sp", hg[:, j, 0:n], W[nm][l:l + 1, off:off + n].partition_broadcast(128), (), [("c", "hg")])
            ts("dve", hg[:, 8, :], hg[:, 8, :], 1.0 - lam_init, ALU.mult, [("c", "hg")], [("c", "hg")])

        def rmsnorm(l, which):
            carv.off = UN - 4096
            sqb = [carv.take(512, F32) for _ in range(2)]
            rstd = carv.take(CH, F32)
            for th in range(2):
                tsl = slice(th * 512, (th + 1) * 512)
                pk, ps = psum()
                for c in range(16):
                    sk, sq_ = ("sq", c % 2), sqb[c % 2]
                    act(sq_, xT[:, c, tsl], AF.Square, [("x", c, th)], [sk])
                    S.op("pe", (lambda e, c=c, sq_=sq_, ps=ps: e.matmul(ps[:], lhsT=onesf[:], rhs=sq_, start=(c == 0), stop=(c == 15))),
                         [sk, ("c", "onesf")], [pk])
                act(rstd[:, tsl], ps[:], AF.Sqrt, [pk, ("c", "eps")], [("rstd", th)], bias=epsT[:], scale=1.0 / D)
                recip(rstd[:, tsl], rstd[:, tsl], [("rstd", th)], [("rstd", th)])
                for c in range(16):
                    stt(hT[:, c, tsl], xT[:, c, tsl], gains[:, l, which, c:c + 1], rstd[:, tsl], ALU.mult, ALU.mult,
                        [("x", c, th), ("rstd", th), ("c", "gains")], [("h", c, th)])

        def ffn(l, pre):
            Wg, Wu, Wd = W[pre + "_w_gate"][l], W[pre + "_w_up"][l], W[pre + "_w_down"][l]
            carv.reset()
            actT = carv.take(12 * CH).rearrange("p (j t) -> p j t", t=CH)
            sgl = [carv.take(512, F32) for _ in range(2)]
            f0 = 0
            for gi, gs in enumerate(FGROUPS):
                akey = ("actT", gi % 1)
                for jp in range(gs // 2):
                    fc = f0 + 2 * jp
                    slot, wk = wnext([(0, (16, 256), wsrc(Wg, 0, 16, fc * 128, 256)),
                                      (4096, (16, 256), wsrc(Wu, 0, 16, fc * 128, 256))])
                    wg = wview(slot, 0, 16, 256)
                    wu = wview(slot, 4096, 16, 256)
                    for j2 in range(2):
                        j = 2 * jp + j2
                        for th in range(2):
                            tsl = slice(th * 512, (th + 1) * 512)
                            hreads = [("h", k, th) for k in range(16)] + [wk]
                            pkg, psg = psum()
                            mm(psg[:], [(wg[:, k, j2 * 128:(j2 + 1) * 128], hT[:, k, tsl]) for k in range(16)], hreads, [pkg])
                            pku, psu = psum()
                            mm(psu[:], [(wu[:, k, j2 * 128:(j2 + 1) * 128], hT[:, k, tsl]) for k in range(16)], hreads, [pku])
                            si = (j * 2 + th) % 2
                            act(sgl[si], psg[:], AF.Silu, [pkg], [("sgl", si)])
                            tt("dve", actT[:, j, tsl], sgl[si], psu[:], ALU.mult, [("sgl", si), pku], [("actT", j, th)])
                for cq in range(4):
                    slot, wk = wnext([(0, (gs, 512), wsrc(Wd, f0 * 128, gs, cq * 512, 512))])
                    wd = wview(slot, 0, gs, 512)
                    for c4 in range(4):
                        c = cq * 4 + c4
                        for th in range(2):
                            tsl = slice(th * 512, (th + 1) * 512)
                            pk, ps = psum()
                            mm(ps[:], [(wd[:, j, c4 * 128:(c4 + 1) * 128], actT[:, j, tsl]) for j in range(gs)],
                               [("actT", j, th) for j in range(gs)] + [wk], [pk])
                            stt(xT[:, c, tsl], ps[:], 0.5, xT[:, c, tsl], ALU.mult, ALU.add, [pk, ("x", c, th)], [("x", c, th)])
                f0 += gs

        def qk_post(ps, pk, ncols, hd, gain, rope, outbf, okey, gt, qn, qnk, sqt, sqk):
            nh = ncols // hd
            psv = ps[:, 0:ncols].rearrange("p (h d) -> p h d", d=hd)
            qv = qn[:, 0:ncols].rearrange("p (h d) -> p h d", d=hd)
            ov = outbf[:, 0:ncols].rearrange("p (h d) -> p h d", d=hd)
            if gain is not None:
                act(sqt[:, 0:ncols], ps[:, 0:ncols], AF.Square, [pk], [sqk])
                k1, s1 = small()
                rsum(s1[:, 0:nh], sqt[:, 0:ncols].rearrange("p (h d) -> p h d", d=hd), [sqk], [k1])
                act(s1[:, 0:nh], s1[:, 0:nh], AF.Sqrt, [k1, ("c", "eps")], [k1], bias=epsT[:], scale=1.0 / hd)
                recip(s1[:, 0:nh], s1[:, 0:nh], [k1], [k1])
                tt("dve", qv, psv, s1[:, 0:nh].unsqueeze(2).to_broadcast([128, nh, hd]), ALU.mult, [pk, k1], [qnk])
                tt("pool", qv, qv, gain.unsqueeze(1).to_broadcast([128, nh, hd]), ALU.mult, [qnk, ("c", "hg")], [qnk])
            else:
                cp("act", qn[:, 0:ncols], ps[:, 0:ncols], [pk], [qnk])
            cp("pool", outbf[:, 0:ncols], qn[:, 0:ncols], [qnk], [okey])
            if rope is not None:
                half, coff = (8, 0) if rope == "A" else (16, 8)
                cs = cosT[:, gt, coff:coff + half].unsqueeze(1).to_broadcast([128, nh, half])
                sn = sinT[:, gt, coff:coff + half].unsqueeze(1).to_broadcast([128, nh, half])
                x1 = qv[:, :, 0:half]
                x2 = qv[:, :, half:2 * half]
                ka, ta = small()
                kb, tb = small()
                tav = ta[:, 0:nh * half].rearrange("p (h d) -> p h d", d=half)
                tbv = tb[:, 0:nh * half].rearrange("p (h d) -> p h d", d=half)
                RK = [qnk, ("c", "rope")]
                tt("dve", tav, x1, cs, ALU.mult, RK, [ka])
                tt("dve", tbv, x2, sn, ALU.mult, RK, [kb])
                tt("dve", ov[:, :, 0:half], tav, tbv, ALU.subtract, [ka, kb], [okey])
                kc_, tc_ = small()
                kd_, td_ = small()
                tcv = tc_[:, 0:nh * half].rearrange("p (h d) -> p h d", d=half)
                tdv = td_[:, 0:nh * half].rearrange("p (h d) -> p h d", d=half)
                tt("dve", tcv, x2, cs, ALU.mult, RK, [kc_])
                tt("dve", tdv, x1, sn, ALU.mult, RK, [kd_])
                tt("dve", ov[:, :, half:2 * half], tcv, tdv, ALU.add, [kc_, kd_], [okey])

        def gelu_tanh(out, in_, n, reads, writes):
            k1, t1 = small()
            tt("dve", t1[:, 0:n], in_, in_, ALU.mult, reads, [k1])
            ts("dve", t1[:, 0:n], t1[:, 0:n], 0.044715, ALU.mult, [k1], [k1], s2=1.0, op1=ALU.add)
            tt("dve", t1[:, 0:n], t1[:, 0:n], in_, ALU.mult, [k1] + reads, [k1])
            act(t1[:, 0:n], t1[:, 0:n], AF.Sigmoid, [k1], [k1], scale=2.0 * math.sqrt(2.0 / math.pi))
            tt("dve", out, t1[:, 0:n], in_, ALU.mult, [k1] + reads, writes)

        def projection(l, c):
            Win = W["w_in"][l]
            carv.reset()
            stage = [carv.take(4 * CH).rearrange("p (b t) -> p b t", t=CH) for _ in range(2)]
            vstage = carv.take(TPC * 512).rearrange("p (t c) -> p t c", c=512)
            qn2 = [carv.take(512, F32) for _ in range(2)]
            sq2 = [carv.take(512, F32) for _ in range(2)]
            ob2 = [carv.take(512) for _ in range(2)]
            gmst = [carv.take(CH) for _ in range(2)]
            w2b = carv.take(256).rearrange("p (k c) -> p k c", c=128)
            kcn_t = carv.take(128)
            sqc_t = carv.take(128, F32)
            kcTs = carv.take(64)
            vcs = carv.take(128)
            h1bt = carv.take(64)
            cnt = {"i": 0, "stage": 0}
            tok0 = c * CH
            dq = []

            def defer(fn):
                dq.append(fn)
                while len(dq) > 2:
                    dq.pop(0)()

            def flush():
                while dq:
                    dq.pop(0)()

            def proj_tile(slot, wk, t, ncols):
                pk, ps = psum()
                wv = wview(slot, 0, 16, 512)
                mm(ps[:, 0:ncols], [(hT[:, k, t * 128:(t + 1) * 128], wv[:, k, 0:ncols]) for k in range(16)],
                   [("h", k, t // 4) for k in range(16)] + [wk], [pk])
                return pk, ps

            def transposes(outbf, okey, c0, nblk, stg, skey, b0, t):
                bk, pb = psumb()
                for b in range(nblk):
                    tr(pb[:, b * 128:(b + 1) * 128], outbf[:, c0 + b * 128:c0 + (b + 1) * 128], identb[:], [okey, ("c", "ident")], [bk])
                cp("act", stg[:, b0:b0 + nblk, t * 128:(t + 1) * 128],
                   pb[:, 0:nblk * 128].rearrange("p (b t) -> p b t", t=128), [bk], [skey])

            def group(col0, ncols_k, hd, gidx, rope, dest_rows, vdest):
                si = cnt["stage"] % 2
                cnt["stage"] += 1
                stg, skey = stage[si], ("stage", si)
                slot, wk = wnext([(0, (16, 512), wsrc(Win, 0, 16, col0, 512))])
                for t in range(TPC):
                    gt = c * TPC + t
                    pk, ps = proj_tile(slot, wk, t, 512)

                    def post(t=t, gt=gt, pk=pk, ps=ps):
                        i = cnt["i"] % 2
                        cnt["i"] += 1
                        if ncols_k:
                            gain = None if gidx is None else hg[:, gidx, 0:hd]
                            qk_post(ps, pk, ncols_k, hd, gain, rope, ob2[i], ("ob", i), gt, qn2[i], ("qn", i), sq2[i], ("sq2", i))
                            transposes(ob2[i], ("ob", i), 0, ncols_k // 128, stg, skey, 0, t)
                        if ncols_k < 512:
                            cp("act", vstage[:, t, ncols_k:512], ps[:, ncols_k:512], [pk], [("vstage",)])
                    defer(post)

                def fin():
                    if ncols_k:
                        dst, dkey = dest_rows
                        nb = ncols_k // 128
                        dma("sp", dst.rearrange("(b p) t -> p b t", p=128), stg[:, 0:nb, :], [skey], [dkey])
                    if ncols_k < 512:
                        dst, dkey = vdest
                        dma("sp", dst.rearrange("(t p) c -> p t c", p=128), vstage[:, :, ncols_k:512], [("vstage",)], [dkey])
                dq.append(fin)

            tsl = slice(tok0, tok0 + CH)
            rsl = slice(tok0, tok0 + CH)
            group(COL["aq"], 512, 64, 0, "A", (QT_d[0:512, :], ("QT", 0)), None)
            group(COL["ak"], 512, 64, 1, "A", (KTa_d[l][:, tsl], ("KTa", l, c)), None)
            group(COL["av"], 0, 0, None, None, None, (Va_d[l][rsl, :], ("Va", l, c)))
            group(COL["bq"], 512, 128, 2, "B", (QT_d[512:1024, :], ("QT", 1)), None)
            group(COL["bq"] + 512, 512, 128, 2, "B", (QT_d[1024:1536, :], ("QT", 2)), None)
            group(COL["bkv"] + 512, 256, 128, 4, "B", (KTs_d[l][:, tsl], ("KTs", l, c)), (Vs_d[l][rsl, :], ("Vs", l, c)))
            group(COL["bkv"] + 1024, 256, 128, 5, "B", (KTw_d[l][:, tsl], ("KTw", l, c)), (Vw_d[l][rsl, :], ("Vw", l, c)))
            group(COL["cq"], 512, 128, 6, None, (QT_d[1536:2048, :], ("QT", 3)), None)
            group(COL["ck"], 512, 128, 7, None, (KTc_d[l][:, tsl], ("KTc", l, c)), None)
            group(COL["cv"], 0, 0, None, None, None, (Vc_d[l][rsl, :], ("Vc", l, c)))

            flush()
            si = cnt["stage"] % 2
            cnt["stage"] += 1
            stg, skey = stage[si], ("stage", si)
            slot, wk = wnext([(0, (16, 512), wsrc(Win, 0, 16, COL["bkv"], 512))])
            for t in range(TPC):
                gt = c * TPC + t
                pk, ps = proj_tile(slot, wk, t, 512)
                i = cnt["i"] % 2
                cnt["i"] += 1
                qk_post(ps, pk, 256, 128, None, "B", ob2[i], ("ob", i), gt, qn2[i], ("qn", i), sq2[i], ("sq2", i))
                cp("act", ob2[i][:, 256:512], ps[:, 256:512], [pk], [("ob", i)])
                transposes(ob2[i], ("ob", i), 0, 4, stg, skey, 0, t)
            for b in range(4):
                kv = b // 2
                sv = stg[:, b, :].rearrange("p (n l) -> p n l", l=64)
                tt("dve", sv, sv, posT[:, l, kv, :].unsqueeze(1).to_broadcast([128, CH // 64, 64]), ALU.add,
                   [skey, ("c", "posT")], [skey])
            nb_c = CH // 64
            if DBG["stop"] and l == 0 and c == 0:
                dma("sp", dbgstg_d[:, :], stg[:].rearrange("p b t -> p (b t)"), [skey], [("dbgstg",)])
            for kv in range(2):
                w1 = W["nsa_cmp_w1"][l, kv]
                slot, wk = wnext([(0, (64, 128), w1.rearrange("(k p) c -> p k c", p=128))])
                w1v = wview(slot, 0, 64, 128)
                pk, ps = psum()
                src = stg[:, 2 * kv:2 * kv + 2, :].rearrange("p g (n l) -> p g n l", l=64)
                mm(ps[:, 0:2 * nb_c], [(w1v[:, li, :], src[:, :, :, li]) for li in range(64)], [skey, wk], [pk])
                k1, h1 = small()
                cp("act", h1[:, 0:2 * nb_c], ps[:, 0:2 * nb_c], [pk], [k1])
                if DBG["stop"] and l == 0 and c == 0:
                    dma("sp", dbgh1_d[kv, :, :], h1[:, 0:2 * nb_c], [k1], [("dbgh1", kv)])
                k2, h1g = small()
                gelu_tanh(h1g[:, 0:2 * nb_c], h1[:, 0:2 * nb_c], 2 * nb_c, [k1], [k2])
                if l == 0 and c == 0:
                    dbg_dump(f"h1g{kv}", h1g[:, 0:2 * nb_c], [k2], [128, 2 * nb_c])
                h1b = h1bt
                k3 = ("h1bt",)
                cp("dve", h1b[:, 0:2 * nb_c], h1g[:, 0:2 * nb_c], [k2], [k3])
                dma("pool", w2b[:, kv, :], W["nsa_cmp_w2"][l, kv, :, :], (), [("w2b", kv)])
                pk2, ps2 = psum()
                mm(ps2[0:2 * nb_c, 0:128], [(h1b[:, 0:2 * nb_c], w2b[:, kv, :])], [k3, ("w2b", kv)], [pk2])
                if l == 0 and c == 0:
                    dbg_psum(f"ps2_{kv}", ps2[:, 0:64], pk2, 64)
                if kv == 0:
                    k5, sqs = small()
                    kk, kcn = ("kcn",), kcn_t
                    act(sqc_t[0:2 * nb_c, :], ps2[0:2 * nb_c, 0:128], AF.Square, [pk2], [("sqc",)])
                    rsum(sqs[0:2 * nb_c, 0:1], sqc_t[0:2 * nb_c, :], [("sqc",)], [k5])
                    act(sqs[0:2 * nb_c, 0:1], sqs[0:2 * nb_c, 0:1], AF.Sqrt, [k5, ("c", "eps")], [k5], bias=epsT[0:2 * nb_c, :], scale=1.0 / 128)
                    recip(sqs[0:2 * nb_c, 0:1], sqs[0:2 * nb_c, 0:1], [k5], [k5])
                    stt(kcn[0:2 * nb_c, :], ps2[0:2 * nb_c, 0:128], sqs[0:2 * nb_c, 0:1], hg[0:2 * nb_c, 3, :], ALU.mult, ALU.mult,
                        [pk2, k5, ("c", "hg")], [kk])
                    bk, pb = psumb()
                    tr(pb[:, 0:2 * nb_c], kcn[0:2 * nb_c, :], identb[0:2 * nb_c, 0:2 * nb_c], [kk, ("c", "ident")], [bk])
                    cp("act", kcTs[:, 0:2 * nb_c], pb[:, 0:2 * nb_c], [bk], [("kcTs",)])
                    dma("sp", kcT_d[l][:, :, c * nb_c:(c + 1) * nb_c].rearrange("g d n -> d g n"),
                        kcTs[:, 0:2 * nb_c].rearrange("p (g n) -> p g n", n=nb_c), [("kcTs",)], [("kcT", l, c)])
                else:
                    cp("act", vcs[0:2 * nb_c, :], ps2[0:2 * nb_c, 0:128], [pk2], [("vcs",)])
                    for g in range(2):
                        dma("sp", vcB_d[l][g, c * nb_c:(c + 1) * nb_c, :], vcs[g * nb_c:(g + 1) * nb_c, :], [("vcs",)], [("vcB", l, c)])

            slot, wk = wnext([(0, (16, 24), wsrc(Win, 0, 16, COL["bg"], 24)), (16 * 24, (16, 4), wsrc(Win, 0, 16, COL["cf"], 4))])
            wg_ = wview(slot, 0, 16, 24)
            wf_ = wview(slot, 16 * 24, 16, 4)
            for t in range(TPC):
                gt = c * TPC + t
                pk, ps = psum()
                hreads = [("h", k, t // 4) for k in range(16)] + [wk]
                mm(ps[:, 0:24], [(hT[:, k, t * 128:(t + 1) * 128], wg_[:, k, :]) for k in range(16)], hreads, [pk])
                pkf, psf_ = psum()
                mm(psf_[:, 0:4], [(hT[:, k, t * 128:(t + 1) * 128], wf_[:, k, :]) for k in range(16)], hreads, [pkf])
                act(gates[:, t, :], ps[:, 0:24], AF.Sigmoid, [pk], [("gates", t)])
                k1, z = small()
                tt("dve", z[:, 0:4], psf_[:, 0:4], fbias[:, l, :], ALU.add, [pkf, ("c", "fbias")], [k1])
                act(z[:, 0:4], z[:, 0:4], AF.Exp, [k1], [k1], scale=-1.0)
                act(z[:, 0:4], z[:, 0:4], AF.Ln, [k1], [k1], bias=1.0)
                ts("dve", z[:, 0:4], z[:, 0:4], -1.0, ALU.mult, [k1], [k1])
                pkc, psc = psum()
                mm(psc[:, 0:4], [(trif[:], z[:, 0:4])], [k1, ("c", "trif")], [pkc])
                tt("dve", cum[:, l, gt, :], psc[:, 0:4], carry[:, l, gt, :], ALU.add, [pkc, ("carry", l)], [("cum", l)])
                pkt, pst = psum()
                mm(pst[:, 0:4], [(onesf[:], z[:, 0:4])], [k1, ("c", "onesf")], [pkt])
                tt("dve", carry[:, l, gt + 1, :], pst[:, 0:4], carry[:, l, gt, :], ALU.add, [pkt, ("carry", l)], [("carry", l)])

            for jq in range(12):
                slot, wk = wnext([(0, (16, 512), wsrc(Win, 0, 16, GM0 + jq * 512, 512))])
                wv = wview(slot, 0, 16, 512)
                for j4 in range(4):
                    j = jq * 4 + j4
                    gi = j % 2
                    for th in range(2):
                        tsl2 = slice(th * 512, (th + 1) * 512)
                        pk, ps = psum()
                        mm(ps[:], [(wv[:, k, j4 * 128:(j4 + 1) * 128], hT[:, k, tsl2]) for k in range(16)],
                           [("h", k, th) for k in range(16)] + [wk], [pk])
                        act(gmst[gi][:, tsl2], ps[:], AF.Sigmoid, [pk], [("gmst", gi)])
                    dma("sp", gmT_d[j * 128:(j + 1) * 128, :], gmst[gi], [("gmst", gi)], [("gmT", j)])

        def attention(l, c):
            carv.reset()
            kT2 = [carv.take(SEQ) for _ in range(2)]
            vt2 = [carv.take(NT * 129).rearrange("p (t c) -> p t c", c=129) for _ in range(2)]
            qT2 = [carv.take(CH) for _ in range(2)]
            pT3 = [carv.take(NT * 128).rearrange("p (t q) -> p t q", q=128) for _ in range(2)]
            Msb = carv.take(NT * 128).rearrange("p (t q) -> p t q", q=128)
            qTn = [carv.take(CH) for _ in range(4)]
            ostn = [carv.take(CH) for _ in range(2)]
            osm = [carv.take(128) for _ in range(4)]
            negcum = carv.take(NT * 4, F32).rearrange("p (t h) -> p t h", h=4)
            of2 = [carv.take(128, F32) for _ in range(2)]
            uf2 = [carv.take(128, F32) for _ in range(2)]
            ob2 = [carv.take(128) for _ in range(2)]
            accn = [carv.take(128, F32) for _ in range(4)]
            imp = carv.take(32, F32)
            score = carv.take(32, F32)
            top8 = carv.take(8, F32)
            selb = carv.take(32)
            selTs = carv.take(128)
            pcT = [carv.take(128) for _ in range(2)]
            pcf = [carv.take(128, F32) for _ in range(2)]
            kcT_s = carv.take(64)
            vc_s = carv.take(2 * 129).rearrange("p (g c) -> p g c", c=129)
            ntk = (c + 1) * TPC
            ntok = ntk * 128
            ctr = {"kv": 0, "q": 0, "p": 0, "pw": 0, "o": 0, "os": 0}
            nkeyc = c + 1

            for i in range(2):
                memset("pool", vt2[i][:, :, 128:129], 1.0, [("vt", i)])
            memset("pool", vc_s[:, :, 128:129], 1.0, [("vc_s",)])

            def load_kv(KT, krows, ktag, V, vcols, vtag):
                i = ctr["kv"] % 2
                ctr["kv"] += 1
                dma("sp", kT2[i][:, 0:ntok], KT[krows, 0:ntok], [(ktag, l, cc) for cc in range(nkeyc)], [("kT", i)])
                dma("sp", vt2[i][:, 0:ntk, 0:128], V[0:ntok, vcols].rearrange("(t p) c -> p t c", p=128),
                    [(vtag, l, cc) for cc in range(nkeyc)], [("vt", i)])
                return kT2[i], ("kT", i), vt2[i], ("vt", i)

            def load_q(row0, qtag):
                i = ctr["q"] % 2
                ctr["q"] += 1
                dma("sp", qT2[i][:, :], QT_d[row0:row0 + 128, :], [("QT", qtag)], [("qT", i)])
                return qT2[i], ("qT", i)

            def scores(qT, qkey, kT, kkey, part, t, kts, scale, biasfn=None, maskfn=None, selmask=False, extra_reads=(), pool="p", msb=None, msbkey=None):
                if pool == "w":
                    i = ctr["pw"] % 2
                    ctr["pw"] += 1
                    pT, pkey = pTw[i], ("pTw", i)
                else:
                    i = ctr["p"] % 2
                    ctr["p"] += 1
                    pT, pkey = pT3[i], ("pT", i)
                qs = qT[part, t * 128:(t + 1) * 128]
                for b0 in range(0, len(kts), 4):
                    batch = kts[b0:b0 + 4]
                    pk, ps = psum()

                    def fn(e, batch=batch, ps=ps):
                        ins = None
                        for j, kt in enumerate(batch):
                            ins = e.matmul(ps[:, j * 128:(j + 1) * 128], lhsT=kT[part, kt * 128:(kt + 1) * 128], rhs=qs, start=True, stop=True)
                        return ins
                    S.op("pe", fn, [qkey, kkey], [pk])
                    nb = len(batch)
                    if biasfn is None:
                        act(pT[:, b0:b0 + nb, :], ps[:, 0:nb * 128].rearrange("p (j q) -> p j q", q=128), AF.Exp,
                            [pk], [pkey], scale=scale)
                    else:
                        for j, kt in enumerate(batch):
                            bap, bkey = biasfn(kt)
                            act(pT[:, b0 + j, :], ps[:, j * 128:(j + 1) * 128], AF.Exp, [pk, bkey], [pkey], scale=scale, bias=bap)
                    if selmask:
                        tt("dve", pT[:, b0:b0 + nb, :], pT[:, b0:b0 + nb, :], msb[:, batch[0]:batch[0] + nb, :], ALU.mult,
                           [pkey, msbkey], [pkey])
                    if maskfn is not None:
                        for j, kt in enumerate(batch):
                            m = maskfn(kt)
                            if m is not None:
                                tt("dve", pT[:, b0 + j, :], pT[:, b0 + j, :], m, ALU.mult, [pkey, ("c", "masks")], [pkey])
                return pT, pkey

            def pv(pT, pkey, vt, vkey, kts, ncol=129):
                pk, ps = psum()
                mm(ps[:, 0:ncol], [(pT[:, j, :], vt[:, kt, 0:ncol]) for j, kt in enumerate(kts)], [pkey, vkey], [pk])
                return pk, ps

            def out_T(obf, okey, chunk_idx, t):
                bk, pb = psumb()
                tr(pb[:, 0:128], obf, identb[:], [okey, ("c", "ident")], [bk])
                return bk, pb

            def head_out_begin():
                i = ctr["os"] % 2
                ctr["os"] += 1
                return ostn[i], ("ostn", i)

            def head_out_end(stg, skey, chunk_idx):
                dma("sp", oT_d[chunk_idx * 128:(chunk_idx + 1) * 128, :], stg, [skey], [("oT", chunk_idx)])

            causal = lambda gt: (lambda kt: trib[:] if kt == gt else None)

            sc_a = 64 ** -0.5
            def run_pipeline(units):
                prev = None
                for sfn, ffn in units:
                    ctx = sfn()
                    if prev is not None:
                        prev[1](prev[0])
                    prev = (ctx, ffn)
                if prev is not None:
                    prev[1](prev[0])

            unitsA = []
            stA = {}
            for h in range(4):
                for t in range(TPC):
                    for m in range(2):
                        def sfn(h=h, t=t, m=m):
                            if t == 0 and m == 0:
                                stA["kv"] = load_kv(KTa_d[l], slice(h * 128, (h + 1) * 128), "KTa", Va_d[l], slice(h * 128, (h + 1) * 128), "Va")
                                stA["q"] = load_q(h * 128, 0)
                            kT, kkey, vt, vkey = stA["kv"]
                            qT, qkey = stA["q"]
                            gt = c * TPC + t
                            kts = list(range(gt + 1))
                            part = slice(m * 64, (m + 1) * 64)
                            pT, pkey = scores(qT, qkey, kT, kkey, part, t, kts, sc_a, maskfn=causal(gt))
                            return (pT, pkey, vt, vkey, kts)

                        def ffn_(ctx, h=h, t=t, m=m):
                            pT, pkey, vt, vkey, kts = ctx
                            res = pv(pT, pkey, vt, vkey, kts)
                            if m == 0:
                                if t == 0:
                                    stA["stg"] = head_out_begin()
                                stA["r0"] = res
                                return
                            (pk0, ps0), (pk1, ps1) = stA["r0"], res
                            stg, skey = stA["stg"]
                            k1, r = small()
                            recip(r[:, 0:1], ps0[:, 128:129], [pk0], [k1])
                            recip(r[:, 1:2], ps1[:, 128:129], [pk1], [k1])
                            tt("dve", r[:, 1:2], r[:, 1:2], lamt[:, l, 1:2], ALU.mult, [k1, ("c", "lamt")], [k1])
                            i = ctr["o"] % 2
                            ctr["o"] += 1
                            ts("dve", uf2[i], ps0[:, 0:128], r[:, 0:1], ALU.mult, [pk0, k1], [("uf", i)])
                            stt(of2[i], ps1[:, 0:128], r[:, 1:2], uf2[i], ALU.mult, ALU.add, [pk1, k1, ("uf", i)], [("of", i)])
                            act(uf2[i], of2[i], AF.Square, [("of", i)], [("uf", i)])
                            k2, s2_ = small()
                            rsum(s2_[:, 0:1], uf2[i], [("uf", i)], [k2])
                            act(s2_[:, 0:1], s2_[:, 0:1], AF.Sqrt, [k2, ("c", "eps")], [k2], bias=epsT[:], scale=1.0 / 128)
                            recip(s2_[:, 0:1], s2_[:, 0:1], [k2], [k2])
                            stt(ob2[i], of2[i], s2_[:, 0:1], hg[:, 8, :], ALU.mult, ALU.mult, [("of", i), k2, ("c", "hg")], [("obo", i)])
                            bk, pb = out_T(ob2[i], ("obo", i), h, t)
                            cp("act", stg[:, t * 128:(t + 1) * 128], pb[:, 0:128], [bk], [skey])
                            if t == TPC - 1:
                                head_out_end(stg, skey, h)
                        unitsA.append((sfn, ffn_))
            run_pipeline(unitsA)

            if DBG.get("apart") == "A":
                return
            sc_c = 128 ** -0.5
            ts("dve", negcum[:, 0:ntk, :], cum[:, l, 0:ntk, :], -1.0, ALU.mult, [("cum", l)], [("negcum",)])
            if l == 0 and c == 0:
                dbg_dump("cum", cum[:, l, :, :], [("cum", l)], [128, NT, 4])
                dbg_dump("carry", carry[:, l, :, :], [("carry", l)], [128, NT + 1, 4])
                dbg_dump("gates", gates[:], [("gates", t_) for t_ in range(TPC)], [128, TPC, 24])
                dbg_dump("lamt", lamt[:], [("c", "lamt")], [128, DEPTH, 4])
                dbg_dump("cosT", cosT[:], [("c", "rope")], [128, NT, 24])
            unitsC = []
            stC = {}
            for h in range(4):
                for t in range(TPC):
                    def sfn(h=h, t=t):
                        if t == 0:
                            stC["kv"] = load_kv(KTc_d[l], slice(h * 128, (h + 1) * 128), "KTc", Vc_d[l], slice(h * 128, (h + 1) * 128), "Vc")
                            stC["q"] = load_q(1536 + h * 128, 3)
                        kT, kkey, vt, vkey = stC["kv"]
                        qT, qkey = stC["q"]
                        gt = c * TPC + t
                        kts = list(range(gt + 1))
                        kb_, bt = small()
                        ts("dve", bt[:, 0:gt + 1], negcum[:, 0:gt + 1, h], carry[:, l, gt + 1, h:h + 1], ALU.add,
                           [("negcum",), ("carry", l)], [kb_])
                        pT, pkey = scores(qT, qkey, kT, kkey, slice(0, 128), t, kts, sc_c,
                                          biasfn=lambda kt, bt=bt, kb_=kb_: (bt[:, kt:kt + 1], kb_), maskfn=causal(gt))
                        return (pT, pkey, vt, vkey, kts)

                    def ffn_(ctx, h=h, t=t):
                        pT, pkey, vt, vkey, kts = ctx
                        if t == 0:
                            stC["stg"] = head_out_begin()
                        stg, skey = stC["stg"]
                        pk0, ps0 = pv(pT, pkey, vt, vkey, kts)
                        k1, r = small()
                        recip(r[:, 0:1], ps0[:, 128:129], [pk0], [k1])
                        i = ctr["o"] % 2
                        ctr["o"] += 1
                        ts("dve", ob2[i], ps0[:, 0:128], r[:, 0:1], ALU.mult, [pk0, k1], [("obo", i)])
                        bk, pb = out_T(ob2[i], ("obo", i), 12 + h, t)
                        cp("act", stg[:, t * 128:(t + 1) * 128], pb[:, 0:128], [bk], [skey])
                        if t == TPC - 1:
                            head_out_end(stg, skey, 12 + h)
                    unitsC.append((sfn, ffn_))
            run_pipeline(unitsC)

            if DBG.get("apart") == "C":
                return
            S.barrier()
            carv.reset()
            kT2 = [carv.take(SEQ) for _ in range(2)]
            vt2 = [carv.take(NT * 129).rearrange("p (t c) -> p t c", c=129) for _ in range(2)]
            qTn = [carv.take(CH) for _ in range(4)]
            pT3 = [carv.take(NT * 128).rearrange("p (t q) -> p t q", q=128) for _ in range(2)]
            pTw = [carv.take(5 * 128).rearrange("p (t q) -> p t q", q=128) for _ in range(2)]
            Msb2 = [carv.take(NT * 128).rearrange("p (t q) -> p t q", q=128) for _ in range(2)]
            ob2 = [carv.take(128) for _ in range(2)]
            accn2 = [[carv.take(128, F32) for _ in range(4)] for _ in range(2)]
            imp2 = [carv.take(32, F32) for _ in range(2)]
            score2 = [carv.take(32, F32) for _ in range(2)]
            top82 = [carv.take(8, F32) for _ in range(2)]
            selb2 = [carv.take(32) for _ in range(2)]
            selTs2 = [carv.take(128) for _ in range(2)]
            pcT = [carv.take(128) for _ in range(4)]
            pcf = [carv.take(128, F32) for _ in range(4)]
            kcT_s = carv.take(64)
            vc_s = carv.take(2 * 129).rearrange("p (g c) -> p g c", c=129)
            osm = [carv.take(128) for _ in range(4)]
            for i in range(2):
                memset("pool", vt2[i][:, :, 128:129], 1.0, [("vt", i)])
            memset("pool", vc_s[:, :, 128:129], 1.0, [("vc_s",)])
            sc_b = 128 ** -0.5
            nblk = (c + 1) * (CH // 64)
            dma("sp", kcT_s[:, 0:2 * nblk].rearrange("p (g n) -> p g n", n=nblk), kcT_d[l][:, :, 0:nblk].rearrange("g d n -> d g n"),
                [("kcT", l, cc) for cc in range(nkeyc)], [("kcT_s",)])
            dma("sp", vc_s[0:nblk, :, 0:128], vcB_d[l][:, 0:nblk, :].rearrange("g n d -> n g d"),
                [("vcB", l, cc) for cc in range(nkeyc)], [("vc_s",)])
            for g in range(2):
                kTs, kskey, vts, vskey = load_kv(KTs_d[l], slice(g * 128, (g + 1) * 128), "KTs", Vs_d[l], slice(g * 128, (g + 1) * 128), "Vs")
                kTw, kwkey, vtw, vwkey = load_kv(KTw_d[l], slice(g * 128, (g + 1) * 128), "KTw", Vw_d[l], slice(g * 128, (g + 1) * 128), "Vw")
                qh = []
                for hh in range(4):
                    qh.append((qTn[hh], ("qTn", hh)))
                    dma("sp", qTn[hh][:, :], QT_d[512 + (g * 4 + hh) * 128:512 + (g * 4 + hh + 1) * 128, :],
                        [("QT", 1 + g)], [("qTn", hh)])
                def stage_ab(t, g=g, qh=qh):
                    gt = c * TPC + t
                    par = t % 2
                    Msb, accn, imp, score, top8, selb, selTs = Msb2[par], accn2[par], imp2[par], score2[par], top82[par], selb2[par], selTs2[par]
                    KM, KA, KI, KS, KT8, KSB, KST = ("Msb", par), ("accn", par), ("imp", par), ("score", par), ("top8", par), ("selb", par), ("selTs", par)
                    for hh in range(4):
                        qT, qkey = qh[hh]
                        pk, ps = psum()
                        mm(ps[0:nblk, 0:128], [(kcT_s[:, g * nblk:(g + 1) * nblk], qT[:, t * 128:(t + 1) * 128])], [("kcT_s",), qkey], [pk])
                        act(pcf[hh][0:nblk, :], ps[0:nblk, 0:128], AF.Exp, [pk], [("pcf", hh)], scale=sc_b)
                        tt("dve", pcT[hh][0:nblk, :], pcf[hh][0:nblk, :], cmaskb[0:nblk, gt * 128:(gt + 1) * 128], ALU.mult,
                           [("pcf", hh), ("c", "masks")], [("pcT", hh)])
                    for hh in range(4):
                        h = g * 4 + hh
                        pko, pso = psum()
                        mm(pso[:, 0:129], [(pcT[hh][0:nblk, :], vc_s[0:nblk, g, :])], [("pcT", hh), ("vc_s",)], [pko])
                        pkp, psp = psum()
                        mm(psp[:, 0:nblk], [(pcT[hh][0:nblk, :], identb[0:nblk, 0:nblk])], [("pcT", hh), ("c", "ident")], [pkp])
                        k1, r = small()
                        ts("dve", r[:, 0:1], pso[:, 128:129], 1e-30, ALU.add, [pko], [k1])
                        recip(r[:, 0:1], r[:, 0:1], [k1], [k1])
                        tt("dve", r[:, 1:2], r[:, 0:1], gates[:, t, h * 3:h * 3 + 1], ALU.mult, [k1, ("gates", t)], [k1])
                        ts("dve", accn[hh], pso[:, 0:128], r[:, 1:2], ALU.mult, [pko, k1], [(KA, hh)])
                        if hh == 0:
                            if nblk < 32:
                                memset("dve", imp[:, nblk:32], 0.0, [KI])
                            ts("dve", imp[:, 0:nblk], psp[:, 0:nblk], r[:, 0:1], ALU.mult, [pkp, k1], [KI])
                        else:
                            stt(imp[:, 0:nblk], psp[:, 0:nblk], r[:, 0:1], imp[:, 0:nblk], ALU.mult, ALU.add, [pkp, k1, KI], [KI])
                    tt("dve", score, imp, selA[:, gt, :], ALU.mult, [KI, ("c", "selA")], [KS])
                    tt("dve", score, score, selB[:, gt, :], ALU.add, [KS, ("c", "selB")], [KS])
                    S.op("dve", lambda e, top8=top8, score=score: e.max(out=top8, in_=score), [KS], [KT8])
                    ts("dve", selb, score, top8[:, 7:8], ALU.is_ge, [KS, KT8], [KSB])
                    bk, pb = psumb()
                    tr(pb[0:32, 0:128], selb, identb[:], [KSB, ("c", "ident")], [bk])
                    cp("act", selTs[0:32, :], pb[0:32, 0:128], [bk], [KST])
                    kts = list(range(gt + 1))
                    for b0_ in range(0, len(kts), 4):
                        batch = kts[b0_:b0_ + 4]
                        pk, ps = psum()

                        def fn(e, batch=batch, ps=ps, selTs=selTs):
                            ins = None
                            for j, kt in enumerate(batch):
                                ins = e.matmul(ps[:, j * 128:(j + 1) * 128], lhsT=Eb[0:32, kt * 128:(kt + 1) * 128], rhs=selTs[0:32, :], start=True, stop=True)
                            return ins
                        S.op("pe", fn, [KST, ("c", "masks")], [pk])
                        nb = len(batch)
                        cp("act", Msb[:, b0_:b0_ + nb, :], ps[:, 0:nb * 128].rearrange("p (j q) -> p j q", q=128), [pk], [KM])
                    tt("dve", Msb[:, gt, :], Msb[:, gt, :], trib[:], ALU.mult, [KM, ("c", "masks")], [KM])

                def stage_c(t, g=g, qh=qh):
                    gt = c * TPC + t
                    par = t % 2
                    Msb, accn = Msb2[par], accn2[par]
                    KM, KA = ("Msb", par), ("accn", par)
                    kts = list(range(gt + 1))
                    wk0 = max(0, gt - 4)
                    wkts = list(range(wk0, gt + 1))

                    def wmask(kt, gt=gt):
                        if kt == gt:
                            return trib[:]
                        if kt == gt - 4:
                            return bandb[:]
                        return None
                    unitsB = []
                    stB = {}
                    for hh in range(4):
                        def s_sel(hh=hh, t=t, kts=kts):
                            qT, qkey = qh[hh]
                            return scores(qT, qkey, kTs, kskey, slice(0, 128), t, kts, sc_b, selmask=True, msb=Msb, msbkey=KM)

                        def f_sel(ctx, hh=hh, kts=kts):
                            stB[hh] = pv(ctx[0], ctx[1], vts, vskey, kts)

                        def s_win(hh=hh, t=t, wkts=wkts, wmask=wmask):
                            qT, qkey = qh[hh]
                            return scores(qT, qkey, kTw, kwkey, slice(0, 128), t, wkts, sc_b, maskfn=wmask, pool="w")

                        def f_win(ctx, hh=hh, t=t, wkts=wkts, g=g):
                            h = g * 4 + hh
                            pkw, psw = pv(ctx[0], ctx[1], vtw, vwkey, wkts)
                            pks, pss = stB[hh]
                            k1, r = small()
                            recip(r[:, 0:1], pss[:, 128:129], [pks], [k1])
                            recip(r[:, 1:2], psw[:, 128:129], [pkw], [k1])
                            tt("dve", r[:, 0:2], r[:, 0:2], gates[:, t, h * 3 + 1:h * 3 + 3], ALU.mult, [k1, ("gates", t)], [k1])
                            stt(accn[hh], pss[:, 0:128], r[:, 0:1], accn[hh], ALU.mult, ALU.add, [pks, k1, (KA, hh)], [(KA, hh)])
                            i = ctr["o"] % 2
                            ctr["o"] += 1
                            stt(ob2[i], psw[:, 0:128], r[:, 1:2], accn[hh], ALU.mult, ALU.add, [pkw, k1, (KA, hh)], [("obo", i)])
                            bk, pb = out_T(ob2[i], ("obo", i), 4 + h, t)
                            oi = ctr["os"] % 4
                            ctr["os"] += 1
                            cp("act", osm[oi], pb[:, 0:128], [bk], [("osm", oi)])
                            dma("sp", oT_d[(4 + h) * 128:(5 + h) * 128, t * 128:(t + 1) * 128], osm[oi], [("osm", oi)], [("oT", 4 + h)])
                        unitsB += [(s_sel, f_sel), (s_win, f_win)]
                    run_pipeline(unitsB)

                stage_ab(0)
                for t in range(TPC):
                    if t + 1 < TPC:
                        stage_ab(t + 1)
                    stage_c(t)

        def merge(l):
            carv.reset()
            oT = carv.take(16 * CH).rearrange("p (k t) -> p k t", t=CH)
            gm2 = [carv.take(3 * CH).rearrange("p (i t) -> p i t", t=CH) for _ in range(2)]
            t1 = [carv.take(512, F32) for _ in range(1)]
            t2 = [carv.take(512, F32) for _ in range(1)]
            t3 = [carv.take(512, F32) for _ in range(1)]
            for k in range(16):
                dma("sp", oT[:, k, :], oT_d[k * 128:(k + 1) * 128, :], [("oT", k)], [("oTs", k)])
            Wa, Wb, Wc, Wo = W["w_branch_a"][l], W["w_branch_b"][l], W["w_branch_c"][l], W["w_out"][l]
            n = 0
            for cq in range(4):
                slot, wk = wnext([(0, (4, 512), wsrc(Wa, 0, 4, cq * 512, 512)),
                                  (2048, (8, 512), wsrc(Wb, 0, 8, cq * 512, 512)),
                                  (6144, (4, 512), wsrc(Wc, 0, 4, cq * 512, 512))])
                wa = wview(slot, 0, 4, 512)
                wb = wview(slot, 2048, 8, 512)
                wc = wview(slot, 6144, 4, 512)
                for c4 in range(4):
                    c = cq * 4 + c4
                    gi = c % 2
                    dma("sp", gm2[gi], gmT_d.rearrange("(i r) t -> r i t", i=3)[c * 128:(c + 1) * 128, :, :],
                        [("gmT", i * 16 + c) for i in range(3)], [("gm", gi)])
                    csl = slice(c4 * 128, (c4 + 1) * 128)
                    for th in range(2):
                        tsl = slice(th * 512, (th + 1) * 512)
                        i = 0
                        pka, psa = psum()
                        mm(psa[:], [(wa[:, k, csl], oT[:, k, tsl]) for k in range(4)], [("oTs", k) for k in range(4)] + [wk], [pka])
                        pkb, psb_ = psum()
                        mm(psb_[:], [(wb[:, k, csl], oT[:, 4 + k, tsl]) for k in range(8)], [("oTs", 4 + k) for k in range(8)] + [wk], [pkb])
                        pkc, psc = psum()
                        mm(psc[:], [(wc[:, k, csl], oT[:, 12 + k, tsl]) for k in range(4)], [("oTs", 12 + k) for k in range(4)] + [wk], [pkc])
                        tt("dve", t1[i], psa[:], gm2[gi][:, 0, tsl], ALU.mult, [pka, ("gm", gi)], [("t1", i)])
                        tt("dve", t2[i], psb_[:], gm2[gi][:, 1, tsl], ALU.mult, [pkb, ("gm", gi)], [("t2", i)])
                        tt("dve", t3[i], psc[:], gm2[gi][:, 2, tsl], ALU.mult, [pkc, ("gm", gi)], [("t3", i)])
                        tt("pool", t1[i], t1[i], t2[i], ALU.add, [("t1", i), ("t2", i)], [("t1", i)])
                        tt("pool", hT[:, c, tsl], t1[i], t3[i], ALU.add, [("t1", i), ("t3", i)], [("h", c, th)])
            for cq in range(4):
                slot, wk = wnext([(0, (16, 512), wsrc(Wo, 0, 16, cq * 512, 512))])
                wo = wview(slot, 0, 16, 512)
                for c4 in range(4):
                    c = cq * 4 + c4
                    for th in range(2):
                        tsl = slice(th * 512, (th + 1) * 512)
                        pk, ps = psum()
                        mm(ps[:], [(wo[:, k, c4 * 128:(c4 + 1) * 128], hT[:, k, tsl]) for k in range(16)],
                           [("h", k, th) for k in range(16)] + [wk], [pk])
                        tt("dve", xT[:, c, tsl], ps[:], xT[:, c, tsl], ALU.add, [pk, ("x", c, th)], [("x", c, th)])


        def program():
            outkeys = []

            def dump_x(c):
                for k in range(16):
                    for th in range(2):
                        dma("sp", outT_d[k * 128:(k + 1) * 128, c * CH + th * 512:c * CH + (th + 1) * 512], xT[:, k, th * 512:(th + 1) * 512],
                            [("x", k, th)], [("out", c, k, th)])
                        outkeys.append(("out", c, k, th))

            def check(name, l, c):
                if DBG["stop"] == (name, l, c):
                    dump_x(c)
                    dma("sp", dbgh_d[:, :], hT[:].rearrange("p k t -> p (k t)"), [("h", k, th) for k in range(16) for th in range(2)], [("dbgh",)])
                    outkeys.append(("dbgh",))
                    raise _Stop()

            try:
                setup()
                S.barrier()
                for c in range(NCHUNK):
                    for k in range(16):
                        for th in range(2):
                            dma("sp", xT[:, k, th * 512:(th + 1) * 512], xT_d[k * 128:(k + 1) * 128, c * CH + th * 512:c * CH + (th + 1) * 512],
                                (), [("x", k, th)])
                    for l in range(DEPTH):
                        load_layer(l)
                        rmsnorm(l, 0)
                        ffn(l, "ffn1")
                        rmsnorm(l, 1)
                        projection(l, c)
                        S.barrier()
                        attention(l, c)
                        S.barrier()
                        merge(l)
                        S.barrier()
                        rmsnorm(l, 2)
                        ffn(l, "ffn2")
                    dump_x(c)
            except _Stop:
                pass
            S.barrier()
            S.op("sp", None, outkeys + DBG.get("keys", []), ())

        S.collect = True
        program()
        S.collect = False
        prot["f"] = prot["b"] = 0
        rot["smallf"] = 0
        program()
        S.emit(nc, st)
    return nc, consts


_CACHE = {}


def kernel(**inputs):
    if "nc" not in _CACHE:
        _CACHE["nc"] = build_program()
    nc, consts = _CACHE["nc"]
    x = np.asarray(inputs["x"], dtype=np.float32)
    pos = np.asarray(inputs["positions"], dtype=np.int32)
    B = x.shape[0]
    shared = {}
    for k, v in inputs.items():
        if k in ("x", "positions"):
            continue
        a = np.ascontiguousarray(np.asarray(v, dtype=np.float32))
        if k == "da_lambda":
            a = a.reshape(DEPTH, 256)
        if k == "nsa_k_norm":
            a = a.reshape(DEPTH, 384)
        if k in ("ffn1_norm", "mix_norm", "ffn2_norm"):
            a = np.ascontiguousarray(a.reshape(DEPTH, 16, 128).transpose(0, 2, 1))
        if k == "nsa_cmp_pos":
            a = np.ascontiguousarray(a.transpose(0, 1, 3, 2))
        shared[k] = a
    for k, v in consts.items():
        shared["c_" + k] = v
    in_maps = []
    for b in range(N_ACTIVE):
        m = dict(shared)
        m["xT"] = np.ascontiguousarray(x[b].T)
        m["pos"] = np.ascontiguousarray(pos[b].reshape(NT, 128).T)
        in_maps.append(m)
    res = run_bass_kernel_spmd(nc, in_maps, core_ids=list(range(N_ACTIVE)))
    out = np.stack([np.ascontiguousarray(res.results[b]["outT"].T) for b in range(B)], axis=0)
    return out.astype(np.float32)
```

```python
import contextlib
import math

import numpy as np

import concourse.bass as bass
import concourse.mybir as mybir
from concourse.bass_utils import run_bass_kernel_spmd

F32 = mybir.dt.float32
BF16 = mybir.dt.bfloat16
I32 = mybir.dt.int32
AF = mybir.ActivationFunctionType
ALU = mybir.AluOpType
AX = mybir.AxisListType

D = 2048
SEQ = 2048
DEPTH = 2
FF = 5632
NFC = FF // 128
CH = 1024
NCHUNK = SEQ // CH
TPC = CH // 128
NT = SEQ // 128
EPS = 1e-6
IN_COLS = 11804
GM0 = 5660
COL = dict(aq=0, ak=512, av=1024, bq=1536, bkv=2560, bg=4096, cq=4120, ck=4632, cv=5144, cf=5656)
FGROUPS = [12, 12, 12, 8]
N_ACTIVE = 4
ACTIVE_CORES = (0, 1, 4, 5)
DBG = {"stop": None}


class _Stop(Exception):
    pass


class Op:
    __slots__ = ("eng", "fn", "reads", "writes", "dma", "deps", "signal", "semi", "semval", "idx")


class Sched:
    ENGS = ["pe", "act", "dve", "pool", "sp"]
    NDS = 12

    def __init__(self):
        self.ops = []
        self.collect = False

    def op(self, eng, fn, reads=(), writes=(), dma=False):
        if self.collect:
            return
        o = Op()
        o.eng, o.fn, o.reads, o.writes, o.dma = eng, fn, tuple(reads), tuple(writes), dma
        o.deps = None
        o.signal = False
        o.idx = len(self.ops)
        self.ops.append(o)

    def barrier(self):
        if self.collect:
            return
        o = Op()
        o.eng, o.fn, o.reads, o.writes, o.dma = "barrier", None, (), (), False
        o.idx = len(self.ops)
        self.ops.append(o)

    def analyse(self):
        writers = {}
        readers = {}
        ops = self.ops
        last_on = {}
        dmas_since = []
        pending = {e: set() for e in self.ENGS}
        for o in ops:
            if o.eng == "barrier":
                deps = set(last_on.values()) | set(dmas_since)
                for e in self.ENGS:
                    pending[e] |= deps
                dmas_since = []
                continue
            deps = set()
            if pending[o.eng]:
                deps |= pending[o.eng]
                pending[o.eng] = set()
            for k in o.reads:
                for w in writers.get(k, ()):
                    deps.add(w)
            for k in o.writes:
                for w in writers.get(k, ()):
                    if not (o.dma and ops[w].dma):
                        deps.add(w)
                r = readers.get(k)
                if r is not None:
                    deps |= set(r[0].values())
                    deps |= set(r[1])
            for k in o.reads:
                r = readers.setdefault(k, ({}, []))
                if o.dma:
                    r[1].append(o.idx)
                else:
                    r[0][o.eng] = o.idx
            for k in o.writes:
                r = readers.get(k)
                if r is not None and (r[0] or r[1]):
                    writers[k] = [o.idx]
                    readers[k] = ({}, [])
                else:
                    writers.setdefault(k, []).append(o.idx)
            deps.discard(o.idx)
            need = []
            for d in deps:
                od = ops[d]
                if od.dma or o.dma or od.eng != o.eng or o.eng != "pe":
                    need.append(d)
                    od.signal = True
            o.deps = need
            if o.dma:
                o.signal = True
                dmas_since.append(o.idx)
            else:
                last_on[o.eng] = o.idx

    def emit(self, nc, st):
        self.analyse()
        sems = {e: st.enter_context(nc.semaphore(f"s_{e}")) for e in self.ENGS}
        dsems = {e: [st.enter_context(nc.semaphore(f"d_{e}{i}")) for i in range(self.NDS)] for e in ("sp", "pool", "act")}
        allsems = list(sems.values()) + [s for v in dsems.values() for s in v]
        cnt = {e: 0 for e in self.ENGS}
        dcnt = {e: 0 for e in dsems}
        dval = {e: [0] * self.NDS for e in dsems}
        for o in self.ops:
            if o.eng == "barrier":
                continue
            if o.dma:
                i = dcnt[o.eng] % self.NDS
                dcnt[o.eng] += 1
                dval[o.eng][i] += 16
                o.semi = dsems[o.eng][i]
                o.semval = dval[o.eng][i]
            elif o.signal:
                cnt[o.eng] += 1
                o.semi = sems[o.eng]
                o.semval = cnt[o.eng]
        engmap = {"pe": nc.tensor, "act": nc.scalar, "dve": nc.vector, "pool": nc.gpsimd, "sp": nc.sync}
        with nc.Block() as b0:
            @b0.gpsimd
            def _(g):
                for s in allsems:
                    g.sem_clear(s)
        ops = self.ops
        with nc.Block() as block:
            deco = {"pe": block.tensor, "act": block.scalar, "dve": block.vector, "pool": block.gpsimd, "sp": block.sync}
            for ename in self.ENGS:
                mine = [o for o in ops if o.eng == ename]

                def body(e, mine=mine, ename=ename):
                    seen = {}
                    for o in mine:
                        waits = {}
                        for d in o.deps:
                            od = ops[d]
                            key = id(od.semi)
                            if seen.get(key, 0) >= od.semval:
                                continue
                            if key not in waits or waits[key][1] < od.semval:
                                waits[key] = (od.semi, od.semval)
                        if o.dma and o.semval > 16:
                            key = id(o.semi)
                            if seen.get(key, 0) < o.semval - 16:
                                if key not in waits or waits[key][1] < o.semval - 16:
                                    waits[key] = (o.semi, o.semval - 16)
                        for key, (sm, v) in waits.items():
                            e.wait_ge(sm, v)
                            seen[key] = v
                        if o.fn is None:
                            continue
                        ins = o.fn(e)
                        if o.dma:
                            ins.then_inc(o.semi, 16)
                        elif o.signal:
                            ins.then_inc(o.semi, 1)

                deco[ename](body)


def _consts():
    c = {}
    p = np.arange(128)
    c["ident"] = np.eye(128, dtype=np.float32)
    c["tri"] = (p[:, None] <= p[None, :]).astype(np.float32)
    c["band"] = (p[:, None] > p[None, :]).astype(np.float32)
    t = np.arange(SEQ)
    blk = np.arange(32)
    cm = ((blk[:, None] + 1) * 64 - 1 <= t[None, :]).astype(np.float32)
    cmask = np.zeros((128, SEQ), np.float32)
    cmask[:32] = cm
    c["cmask"] = cmask
    cur = t // 64
    valid = blk[None, :] <= cur[:, None]
    forced = (blk[None, :] == 0) | (blk[None, :] == cur[:, None]) | (blk[None, :] == cur[:, None] - 1)
    A = (valid & ~forced).astype(np.float32)
    B = np.where(forced, 1.0e4, np.where(valid, 0.0, -1.0)).astype(np.float32)
    c["selA"] = A.reshape(NT, 128, 32).transpose(1, 0, 2).reshape(128, NT * 32).copy()
    c["selB"] = B.reshape(NT, 128, 32).transpose(1, 0, 2).reshape(128, NT * 32).copy()
    E = np.zeros((128, NT * 128), np.float32)
    for kt in range(NT):
        for s in range(128):
            E[2 * kt + s // 64, kt * 128 + s] = 1.0
    c["E"] = E
    invf = np.zeros((128, 24), np.float32)
    ia = (np.float32(500000.0) ** (-np.arange(0, 16, 2, dtype=np.float32) / np.float32(16))).astype(np.float32)
    ib = (np.float32(500000.0) ** (-np.arange(0, 32, 2, dtype=np.float32) / np.float32(32))).astype(np.float32)
    invf[:, :8] = ia[None]
    invf[:, 8:] = ib[None]
    c["invf"] = invf
    return c


CONST_F32 = ["ident", "tri", "selA", "selB", "invf"]
CONST_BF = ["ident", "tri", "band", "cmask", "E"]


def build_program():
    nc = bass.Bass("TRN2", target_bir_lowering=False)
    S = Sched()
    dram = {}

    def din(name, shape, dt=F32):
        dram[name] = nc.dram_tensor(name, list(shape), dt, kind="ExternalInput").ap()
        return dram[name]

    def dscr(name, shape, dt=BF16):
        if DBG["stop"]:
            dram[name] = nc.dram_tensor(name, list(shape), dt, kind="ExternalOutput").ap()
        else:
            dram[name] = nc.dram_tensor(name, list(shape), dt).ap()
        return dram[name]

    xT_d = din("xT", [D, SEQ])
    pos_d = din("pos", [128, NT], I32)
    outT_d = nc.dram_tensor("outT", [D, SEQ], F32, kind="ExternalOutput").ap()
    W = {}
    for nm, shp in [("ffn1_norm", [DEPTH, 128, 16]), ("ffn1_w_gate", [DEPTH, D, FF]), ("ffn1_w_up", [DEPTH, D, FF]),
                    ("ffn1_w_down", [DEPTH, FF, D]), ("mix_norm", [DEPTH, 128, 16]), ("w_in", [DEPTH, D, IN_COLS]),
                    ("da_q_norm", [DEPTH, 64]), ("da_k_norm", [DEPTH, 64]), ("da_lambda", [DEPTH, 256]),
                    ("da_out_norm", [DEPTH, 128]), ("nsa_q_norm", [DEPTH, 128]), ("nsa_k_norm", [DEPTH, 3 * 128]),
                    ("nsa_cmp_pos", [DEPTH, 2, 128, 64]), ("nsa_cmp_w1", [DEPTH, 2, 8192, 128]),
                    ("nsa_cmp_w2", [DEPTH, 2, 128, 128]), ("fox_q_norm", [DEPTH, 128]), ("fox_k_norm", [DEPTH, 128]),
                    ("fox_f_bias", [DEPTH, 4]), ("w_branch_a", [DEPTH, 512, D]), ("w_branch_b", [DEPTH, 1024, D]),
                    ("w_branch_c", [DEPTH, 512, D]), ("w_out", [DEPTH, D, D]), ("ffn2_norm", [DEPTH, 128, 16]),
                    ("ffn2_w_gate", [DEPTH, D, FF]), ("ffn2_w_up", [DEPTH, D, FF]), ("ffn2_w_down", [DEPTH, FF, D])]:
        W[nm] = din(nm, shp)
    consts = _consts()
    cd = {k: din("c_" + k, consts[k].shape) for k in consts}

    QT_d = dscr("QT", [2048, CH])
    KTa_d = [dscr(f"KTa{l}", [512, SEQ]) for l in range(DEPTH)]
    KTc_d = [dscr(f"KTc{l}", [512, SEQ]) for l in range(DEPTH)]
    KTs_d = [dscr(f"KTs{l}", [256, SEQ]) for l in range(DEPTH)]
    KTw_d = [dscr(f"KTw{l}", [256, SEQ]) for l in range(DEPTH)]
    Va_d = [dscr(f"Va{l}", [SEQ, 512]) for l in range(DEPTH)]
    Vc_d = [dscr(f"Vc{l}", [SEQ, 512]) for l in range(DEPTH)]
    Vs_d = [dscr(f"Vs{l}", [SEQ, 256]) for l in range(DEPTH)]
    Vw_d = [dscr(f"Vw{l}", [SEQ, 256]) for l in range(DEPTH)]
    kcT_d = [dscr(f"kcT{l}", [2, 128, 32]) for l in range(DEPTH)]
    vcB_d = [dscr(f"vcB{l}", [2, 32, 128]) for l in range(DEPTH)]
    gmT_d = dscr("gmT", [3 * D, CH])
    oT_d = dscr("oTs", [D, CH])
    dbgh_d = dscr("dbgh", [128, 16 * CH]) if DBG["stop"] else None
    dbgstg_d = dscr("dbgstg", [128, 4 * CH]) if DBG["stop"] else None
    dbgh1_d = dscr("dbgh1", [2, 128, 32], F32) if DBG["stop"] else None

    st = contextlib.ExitStack()
    with st:
        def sb(name, shape, dt=F32):
            return st.enter_context(nc.sbuf_tensor("s_" + name, list(shape), dt))

        xT = sb("xT", [128, 16, CH])
        hT = sb("hT", [128, 16, CH], BF16)
        NSLOT = 2
        wsl = [sb(f"wsl{i}", [128, 8192], BF16) for i in range(NSLOT)]
        UN = 27136
        un = sb("union", [128, UN], BF16)
        identb = sb("identb", [128, 128], BF16)
        trib = sb("trib", [128, 128], BF16)
        bandb = sb("bandb", [128, 128], BF16)
        cmaskb = sb("cmaskb", [128, SEQ], BF16)
        Eb = sb("Eb", [128, NT * 128], BF16)
        trif = sb("trif", [128, 128])
        onesf = sb("onesf", [128, 128])
        selA = sb("selA", [128, NT, 32], BF16)
        selB = sb("selB", [128, NT, 32], BF16)
        invf = sb("invf", [128, 24])
        epsT = sb("epsT", [128, 1])
        posi = sb("posi", [128, NT], I32)
        posf = sb("posf", [128, NT])
        cosT = sb("cosT", [128, NT, 24])
        sinT = sb("sinT", [128, NT, 24])
        gains = sb("gains", [128, DEPTH, 3, 16])
        hg = sb("hg", [128, 9, 128])
        fbias = sb("fbias", [128, DEPTH, 4])
        lamt = sb("lamt", [128, DEPTH, 4])
        posT = sb("posT", [128, DEPTH, 2, 64])
        gates = sb("gates", [128, TPC, 24])
        cum = sb("cum", [128, DEPTH, NT, 4])
        carry = sb("carry", [128, DEPTH, NT + 1, 4])
        smallf = [sb(f"smallf{i}", [128, 64]) for i in range(8)]
        rot = {"smallf": 0}

        psf = [st.enter_context(nc.psum_tensor(f"psf{i}", [128, 512], F32)) for i in range(6)]
        psb = [st.enter_context(nc.psum_tensor(f"psb{i}", [128, 1024], BF16)) for i in range(2)]
        prot = {"f": 0, "b": 0}

        def psum():
            i = prot["f"] % 6
            prot["f"] += 1
            return ("psf", i), psf[i]

        def psumb():
            i = prot["b"] % 2
            prot["b"] += 1
            return ("psb", i), psb[i]

        def small():
            i = rot["smallf"] % 8
            rot["smallf"] += 1
            return ("smallf", i), smallf[i]

        class Carver:
            def __init__(self):
                self.off = 0

            def reset(self):
                self.off = 0

            def take(self, nel, dt=BF16):
                n16 = nel * (2 if dt == F32 else 1)
                a = self.off
                self.off += n16
                assert self.off <= UN, ("union overflow", self.off)
                ap = un[:, a:a + n16]
                if dt == F32:
                    ap = ap.bitcast(F32)
                return ap

        carv = Carver()
        ukey = {"n": 0}

        def ukeynew(tag):
            ukey["n"] += 1
            return ("u", tag, ukey["n"])

        def act(out, in_, func, reads, writes, bias=None, scale=None):
            kw = {}
            if bias is not None:
                kw["bias"] = bias
            if scale is not None:
                kw["scale"] = scale
            S.op("act", lambda e: e.activation(out=out, in_=in_, func=func, **kw), reads, writes)

        def tt(eng, out, in0, in1, op, reads, writes):
            S.op(eng, lambda e: e.tensor_tensor(out=out, in0=in0, in1=in1, op=op), reads, writes)

        def ts(eng, out, in0, s1, op0, reads, writes, s2=None, op1=None):
            if op1 is None:
                S.op(eng, lambda e: e.tensor_scalar(out=out, in0=in0, scalar1=s1, scalar2=None, op0=op0), reads, writes)
            else:
                S.op(eng, lambda e: e.tensor_scalar(out=out, in0=in0, scalar1=s1, scalar2=s2, op0=op0, op1=op1), reads, writes)

        def stt(out, in0, scalar, in1, op0, op1, reads, writes):
            S.op("dve", lambda e: e.scalar_tensor_tensor(out=out, in0=in0, scalar=scalar, in1=in1, op0=op0, op1=op1), reads, writes)

        def cp(eng, out, in_, reads, writes):
            if eng == "act":
                S.op("act", lambda e: e.activation(out=out, in_=in_, func=AF.Copy), reads, writes)
            else:
                S.op(eng, lambda e: e.tensor_copy(out=out, in_=in_), reads, writes)

        def recip(out, in_, reads, writes):
            S.op("dve", lambda e: e.reciprocal(out=out, in_=in_), reads, writes)

        def rsum(out, in_, reads, writes):
            S.op("dve", lambda e: e.tensor_reduce(out=out, in_=in_, axis=AX.X, op=ALU.add), reads, writes)

        def mm(out, pairs, reads, writes):
            def fn(e):
                ins = None
                n = len(pairs)
                for i, (l, r) in enumerate(pairs):
                    ins = e.matmul(out, lhsT=l, rhs=r, start=(i == 0), stop=(i == n - 1))
                return ins
            S.op("pe", fn, reads, writes)

        def tr(out, in_, ident, reads, writes):
            S.op("pe", lambda e: e.transpose(out, in_, ident), reads, writes)

        def dma(eng, out, in_, reads, writes):
            S.op(eng, lambda e: e.dma_start(out=out, in_=in_), reads, writes, dma=True)

        dbgd = {}

        def dbg_dump(name, ap, reads, shape, dt=F32):
            if not DBG["stop"]:
                return
            if name not in dbgd:
                dbgd[name] = nc.dram_tensor("dbg_" + name, list(shape), dt, kind="ExternalOutput").ap()
            if S.collect:
                return
            dma("sp", dbgd[name], ap, reads, [("dbgd", name)])
            DBG.setdefault("keys", []).append(("dbgd", name))

        def dbg_psum(name, psap, pk, n):
            if not DBG["stop"]:
                return
            k_, sm_ = small()
            cp("act", sm_[:, 0:n], psap, [pk], [k_])
            dbg_dump(name, sm_[:, 0:n], [k_], [128, n])

        def memset(eng, ap, val, writes):
            S.op(eng, lambda e: e.memset(ap, val), (), writes)

        wstate = {"specs": [], "n": 0, "issued": 0}

        def w_issue(upto):
            while wstate["issued"] <= upto and wstate["issued"] < len(wstate["specs"]):
                i = wstate["issued"]
                slot = i % NSLOT
                for (off, shape3, src) in wstate["specs"][i]:
                    k, c = shape3
                    dst = wsl[slot][:, off:off + k * c].rearrange("p (k c) -> p k c", c=c)
                    dma("pool", dst, src, (), [("w", slot)])
                wstate["issued"] += 1

        def wnext(spec):
            if S.collect:
                wstate["specs"].append(spec)
                return wsl[0], ("w", 0)
            n = wstate["n"]
            wstate["n"] += 1
            w_issue(n + NSLOT - 1)
            slot = n % NSLOT
            return wsl[slot], ("w", slot)

        def wview(slot_t, off, k, c):
            return slot_t[:, off:off + k * c].rearrange("p (k c) -> p k c", c=c)

        def wsrc(w2d, r0, nk, c0, ncol):
            return w2d[r0:r0 + nk * 128, c0:c0 + ncol].rearrange("(k p) c -> p k c", p=128)

        def setup():
            carv.reset()
            ang = carv.take(NT * 24, F32).rearrange("p (t f) -> p t f", f=24)
            kq = carv.take(NT * 24, F32).rearrange("p (t f) -> p t f", f=24)
            kqi = carv.take(NT * 24, F32).bitcast(I32).rearrange("p (t f) -> p t f", f=24)
            rr = carv.take(NT * 24, F32).rearrange("p (t f) -> p t f", f=24)
            msk = carv.take(NT * 24, F32).rearrange("p (t f) -> p t f", f=24)
            lamraw = carv.take(DEPTH * 256, F32).rearrange("p (l f) -> p l f", f=256)
            lamp = carv.take(128, F32)
            dma("pool", identb[:], cd["ident"][:, :], (), [("c", "ident")])
            for nm, t_ in [("tri", trib), ("band", bandb), ("cmask", cmaskb), ("E", Eb)]:
                dma("pool", t_[:], cd[nm][:, :], (), [("c", "masks")])
            dma("sp", trif[:], cd["tri"][:, :], (), [("c", "trif")])
            dma("pool", selA[:], cd["selA"][:, :].rearrange("p (t b) -> p t b", b=32), (), [("c", "selA")])
            dma("pool", selB[:], cd["selB"][:, :].rearrange("p (t b) -> p t b", b=32), (), [("c", "selB")])
            dma("sp", invf[:], cd["invf"][:, :], (), [("c", "invf")])
            dma("sp", posi[:], pos_d[:, :], (), [("c", "posi")])
            memset("dve", onesf[:], 1.0, [("c", "onesf")])
            memset("dve", epsT[:], EPS, [("c", "eps")])
            for l in range(DEPTH):
                for j, nm in enumerate(["ffn1_norm", "mix_norm", "ffn2_norm"]):
                    dma("sp", gains[:, l, j, :], W[nm][l, :, :], (), [("c", "gains")])
                dma("sp", fbias[:, l, :], W["fox_f_bias"][l:l + 1, :].partition_broadcast(128), (), [("c", "fbias")])
                dma("sp", lamraw[:, l, :], W["da_lambda"][l:l + 1, :].partition_broadcast(128), (), [("c", "lamraw")])
                for kv in range(2):
                    dma("sp", posT[:, l, kv, :], W["nsa_cmp_pos"][l, kv, :, :], (), [("c", "posT")])
            for l in range(DEPTH):
                lam_init = 0.8 - 0.6 * math.exp(-0.3 * l)
                tt("dve", lamp[:, 0:64], lamraw[:, l, 0:64], lamraw[:, l, 64:128], ALU.mult, [("c", "lamraw")], [("c", "lamp")])
                tt("dve", lamp[:, 64:128], lamraw[:, l, 128:192], lamraw[:, l, 192:256], ALU.mult, [("c", "lamraw")], [("c", "lamp")])
                rsum(lamt[:, l, 2:4], lamp[:].rearrange("p (a b) -> p a b", b=64), [("c", "lamp")], [("c", "lamt")])
                act(lamt[:, l, 2:4], lamt[:, l, 2:4], AF.Exp, [("c", "lamt")], [("c", "lamt")])
                tt("dve", lamt[:, l, 0:1], lamt[:, l, 2:3], lamt[:, l, 3:4], ALU.subtract, [("c", "lamt")], [("c", "lamt")])
                ts("dve", lamt[:, l, 0:1], lamt[:, l, 0:1], lam_init, ALU.add, [("c", "lamt")], [("c", "lamt")])
                ts("dve", lamt[:, l, 1:2], lamt[:, l, 0:1], -1.0, ALU.mult, [("c", "lamt")], [("c", "lamt")])
            R = [("c", "rope")]
            cp("dve", posf[:], posi[:], [("c", "posi")], R)
            tt("dve", ang, posf[:].unsqueeze(2).to_broadcast([128, NT, 24]),
               invf[:].unsqueeze(1).to_broadcast([128, NT, 24]), ALU.mult, R + [("c", "invf")], R)
            C1 = 6.28125
            C2 = 2.0 * math.pi - C1
            for which, outt in ((0, sinT), (1, cosT)):
                src = ang
                if which == 1:
                    ts("dve", msk, ang, math.pi / 2, ALU.add, R, R)
                    cp("dve", ang, msk, R, R)
                ts("dve", kq, src, 1.0 / (2 * math.pi), ALU.mult, R, R, s2=0.5, op1=ALU.add)
                cp("dve", kqi, kq, R, R)
                cp("dve", kq, kqi, R, R)
                stt(rr, kq, -C1, src, ALU.mult, ALU.add, R, R)
                stt(rr, kq, -C2, rr, ALU.mult, ALU.add, R, R)
                ts("dve", msk, rr, math.pi, ALU.is_gt, R, R, s2=-2.0 * math.pi, op1=ALU.mult)
                tt("dve", rr, rr, msk, ALU.add, R, R)
                ts("dve", msk, rr, -math.pi, ALU.is_lt, R, R, s2=2.0 * math.pi, op1=ALU.mult)
                tt("dve", rr, rr, msk, ALU.add, R, R)
                ts("dve", rr, rr, 3.1415925, ALU.min, R, R, s2=-3.1415925, op1=ALU.max)
                act(outt[:], rr, AF.Sin, R, R)
            for l in range(DEPTH):
                memset("dve", carry[:, l, 0, :], 0.0, [("carry", l)])

        def load_layer(l):
            lam_init = 0.8 - 0.6 * math.exp(-0.3 * l)
            for j, (nm, off, n) in enumerate([("da_q_norm", 0, 64), ("da_k_norm", 0, 64), ("nsa_q_norm", 0, 128),
                                              ("nsa_k_norm", 0, 128), ("nsa_k_norm", 128, 128), ("nsa_k_norm", 256, 128),
                                              ("fox_q_norm", 0, 128), ("fox_k_norm", 0, 128), ("da_out_norm", 0, 128)]):
                dma("sp", hg[:, j, 0:n], W[nm][l:l + 1, off:off + n].partition_broadcast(128), (), [("c", "hg")])
            ts("dve", hg[:, 8, :], hg[:, 8, :], 1.0 - lam_init, ALU.mult, [("c", "hg")], [("c", "hg")])

        def rmsnorm(l, which):
            carv.off = UN - 4096
            sqb = [carv.take(512, F32) for _ in range(2)]
            rstd = carv.take(CH, F32)
            for th in range(2):
                tsl = slice(th * 512, (th + 1) * 512)
                pk, ps = psum()
                for c in range(16):
                    sk, sq_ = ("sq", c % 2), sqb[c % 2]
                    act(sq_, xT[:, c, tsl], AF.Square, [("x", c, th)], [sk])
                    S.op("pe", (lambda e, c=c, sq_=sq_, ps=ps: e.matmul(ps[:], lhsT=onesf[:], rhs=sq_, start=(c == 0), stop=(c == 15))),
                         [sk, ("c", "onesf")], [pk])
                act(rstd[:, tsl], ps[:], AF.Sqrt, [pk, ("c", "eps")], [("rstd", th)], bias=epsT[:], scale=1.0 / D)
                recip(rstd[:, tsl], rstd[:, tsl], [("rstd", th)], [("rstd", th)])
                for c in range(16):
                    stt(hT[:, c, tsl], xT[:, c, tsl], gains[:, l, which, c:c + 1], rstd[:, tsl], ALU.mult, ALU.mult,
                        [("x", c, th), ("rstd", th), ("c", "gains")], [("h", c, th)])

        def ffn(l, pre):
            Wg, Wu, Wd = W[pre + "_w_gate"][l], W[pre + "_w_up"][l], W[pre + "_w_down"][l]
            carv.reset()
            actT = carv.take(12 * CH).rearrange("p (j t) -> p j t", t=CH)
            sgl = [carv.take(512, F32) for _ in range(2)]
            f0 = 0
            for gi, gs in enumerate(FGROUPS):
                akey = ("actT", gi % 1)
                for jp in range(gs // 2):
                    fc = f0 + 2 * jp
                    slot, wk = wnext([(0, (16, 256), wsrc(Wg, 0, 16, fc * 128, 256)),
                                      (4096, (16, 256), wsrc(Wu, 0, 16, fc * 128, 256))])
                    wg = wview(slot, 0, 16, 256)
                    wu = wview(slot, 4096, 16, 256)
                    for j2 in range(2):
                        j = 2 * jp + j2
                        for th in range(2):
                            tsl = slice(th * 512, (th + 1) * 512)
                            hreads = [("h", k, th) for k in range(16)] + [wk]
                            pkg, psg = psum()
                            mm(psg[:], [(wg[:, k, j2 * 128:(j2 + 1) * 128], hT[:, k, tsl]) for k in range(16)], hreads, [pkg])
                            pku, psu = psum()
                            mm(psu[:], [(wu[:, k, j2 * 128:(j2 + 1) * 128], hT[:, k, tsl]) for k in range(16)], hreads, [pku])
                            si = (j * 2 + th) % 2
                            act(sgl[si], psg[:], AF.Silu, [pkg], [("sgl", si)])
                            tt("dve", actT[:, j, tsl], sgl[si], psu[:], ALU.mult, [("sgl", si), pku], [("actT", j, th)])
                for cq in range(4):
                    slot, wk = wnext([(0, (gs, 512), wsrc(Wd, f0 * 128, gs, cq * 512, 512))])
                    wd = wview(slot, 0, gs, 512)
                    for c4 in range(4):
                        c = cq * 4 + c4
                        for th in range(2):
                            tsl = slice(th * 512, (th + 1) * 512)
                            pk, ps = psum()
                            mm(ps[:], [(wd[:, j, c4 * 128:(c4 + 1) * 128], actT[:, j, tsl]) for j in range(gs)],
                               [("actT", j, th) for j in range(gs)] + [wk], [pk])
                            stt(xT[:, c, tsl], ps[:], 0.5, xT[:, c, tsl], ALU.mult, ALU.add, [pk, ("x", c, th)], [("x", c, th)])
                f0 += gs

        def qk_post(ps, pk, ncols, hd, gain, rope, outbf, okey, gt, qn, qnk, sqt, sqk):
            nh = ncols // hd
            psv = ps[:, 0:ncols].rearrange("p (h d) -> p h d", d=hd)
            qv = qn[:, 0:ncols].rearrange("p (h d) -> p h d", d=hd)
            ov = outbf[:, 0:ncols].rearrange("p (h d) -> p h d", d=hd)
            if gain is not None:
                act(sqt[:, 0:ncols], ps[:, 0:ncols], AF.Square, [pk], [sqk])
                k1, s1 = small()
                rsum(s1[:, 0:nh], sqt[:, 0:ncols].rearrange("p (h d) -> p h d", d=hd), [sqk], [k1])
                act(s1[:, 0:nh], s1[:, 0:nh], AF.Sqrt, [k1, ("c", "eps")], [k1], bias=epsT[:], scale=1.0 / hd)
                recip(s1[:, 0:nh], s1[:, 0:nh], [k1], [k1])
                tt("dve", qv, psv, s1[:, 0:nh].unsqueeze(2).to_broadcast([128, nh, hd]), ALU.mult, [pk, k1], [qnk])
                tt("pool", qv, qv, gain.unsqueeze(1).to_broadcast([128, nh, hd]), ALU.mult, [qnk, ("c", "hg")], [qnk])
            else:
                cp("act", qn[:, 0:ncols], ps[:, 0:ncols], [pk], [qnk])
            cp("pool", outbf[:, 0:ncols], qn[:, 0:ncols], [qnk], [okey])
            if rope is not None:
                half, coff = (8, 0) if rope == "A" else (16, 8)
                cs = cosT[:, gt, coff:coff + half].unsqueeze(1).to_broadcast([128, nh, half])
                sn = sinT[:, gt, coff:coff + half].unsqueeze(1).to_broadcast([128, nh, half])
                x1 = qv[:, :, 0:half]
                x2 = qv[:, :, half:2 * half]
                ka, ta = small()
                kb, tb = small()
                tav = ta[:, 0:nh * half].rearrange("p (h d) -> p h d", d=half)
                tbv = tb[:, 0:nh * half].rearrange("p (h d) -> p h d", d=half)
                RK = [qnk, ("c", "rope")]
                tt("dve", tav, x1, cs, ALU.mult, RK, [ka])
                tt("dve", tbv, x2, sn, ALU.mult, RK, [kb])
                tt("dve", ov[:, :, 0:half], tav, tbv, ALU.subtract, [ka, kb], [okey])
                kc_, tc_ = small()
                kd_, td_ = small()
                tcv = tc_[:, 0:nh * half].rearrange("p (h d) -> p h d", d=half)
                tdv = td_[:, 0:nh * half].rearrange("p (h d) -> p h d", d=half)
                tt("dve", tcv, x2, cs, ALU.mult, RK, [kc_])
                tt("dve", tdv, x1, sn, ALU.mult, RK, [kd_])
                tt("dve", ov[:, :, half:2 * half], tcv, tdv, ALU.add, [kc_, kd_], [okey])

        def gelu_tanh(out, in_, n, reads, writes):
            k1, t1 = small()
            tt("dve", t1[:, 0:n], in_, in_, ALU.mult, reads, [k1])
            ts("dve", t1[:, 0:n], t1[:, 0:n], 0.044715, ALU.mult, [k1], [k1], s2=1.0, op1=ALU.add)
            tt("dve", t1[:, 0:n], t1[:, 0:n], in_, ALU.mult, [k1] + reads, [k1])
            act(t1[:, 0:n], t1[:, 0:n], AF.Sigmoid, [k1], [k1], scale=2.0 * math.sqrt(2.0 / math.pi))
            tt("dve", out, t1[:, 0:n], in_, ALU.mult, [k1] + reads, writes)

        def projection(l, c):
            Win = W["w_in"][l]
            carv.reset()
            stage = [carv.take(4 * CH).rearrange("p (b t) -> p b t", t=CH) for _ in range(2)]
            vstage = carv.take(TPC * 512).rearrange("p (t c) -> p t c", c=512)
            qn2 = [carv.take(512, F32) for _ in range(2)]
            sq2 = [carv.take(512, F32) for _ in range(2)]
            ob2 = [carv.take(512) for _ in range(2)]
            gmst = [carv.take(CH) for _ in range(2)]
            w2b = carv.take(256).rearrange("p (k c) -> p k c", c=128)
            kcn_t = carv.take(128)
            sqc_t = carv.take(128, F32)
            kcTs = carv.take(64)
            vcs = carv.take(128)
            h1bt = carv.take(64)
            cnt = {"i": 0, "stage": 0}
            tok0 = c * CH
            dq = []

            def defer(fn):
                dq.append(fn)
                while len(dq) > 2:
                    dq.pop(0)()

            def flush():
                while dq:
                    dq.pop(0)()

            def proj_tile(slot, wk, t, ncols):
                pk, ps = psum()
                wv = wview(slot, 0, 16, 512)
                mm(ps[:, 0:ncols], [(hT[:, k, t * 128:(t + 1) * 128], wv[:, k, 0:ncols]) for k in range(16)],
                   [("h", k, t // 4) for k in range(16)] + [wk], [pk])
                return pk, ps

            def transposes(outbf, okey, c0, nblk, stg, skey, b0, t):
                bk, pb = psumb()
                for b in range(nblk):
                    tr(pb[:, b * 128:(b + 1) * 128], outbf[:, c0 + b * 128:c0 + (b + 1) * 128], identb[:], [okey, ("c", "ident")], [bk])
                cp("act", stg[:, b0:b0 + nblk, t * 128:(t + 1) * 128],
                   pb[:, 0:nblk * 128].rearrange("p (b t) -> p b t", t=128), [bk], [skey])

            def group(col0, ncols_k, hd, gidx, rope, dest_rows, vdest):
                si = cnt["stage"] % 2
                cnt["stage"] += 1
                stg, skey = stage[si], ("stage", si)
                slot, wk = wnext([(0, (16, 512), wsrc(Win, 0, 16, col0, 512))])
                for t in range(TPC):
                    gt = c * TPC + t
                    pk, ps = proj_tile(slot, wk, t, 512)

                    def post(t=t, gt=gt, pk=pk, ps=ps):
                        i = cnt["i"] % 2
                        cnt["i"] += 1
                        if ncols_k:
                            gain = None if gidx is None else hg[:, gidx, 0:hd]
                            qk_post(ps, pk, ncols_k, hd, gain, rope, ob2[i], ("ob", i), gt, qn2[i], ("qn", i), sq2[i], ("sq2", i))
                            transposes(ob2[i], ("ob", i), 0, ncols_k // 128, stg, skey, 0, t)
                        if ncols_k < 512:
                            cp("act", vstage[:, t, ncols_k:512], ps[:, ncols_k:512], [pk], [("vstage",)])
                    defer(post)

                def fin():
                    if ncols_k:
                        dst, dkey = dest_rows
                        nb = ncols_k // 128
                        dma("sp", dst.rearrange("(b p) t -> p b t", p=128), stg[:, 0:nb, :], [skey], [dkey])
                    if ncols_k < 512:
                        dst, dkey = vdest
                        dma("sp", dst.rearrange("(t p) c -> p t c", p=128), vstage[:, :, ncols_k:512], [("vstage",)], [dkey])
                dq.append(fin)

            tsl = slice(tok0, tok0 + CH)
            rsl = slice(tok0, tok0 + CH)
            group(COL["aq"], 512, 64, 0, "A", (QT_d[0:512, :], ("QT", 0)), None)
            group(COL["ak"], 512, 64, 1, "A", (KTa_d[l][:, tsl], ("KTa", l, c)), None)
            group(COL["av"], 0, 0, None, None, None, (Va_d[l][rsl, :], ("Va", l, c)))
            group(COL["bq"], 512, 128, 2, "B", (QT_d[512:1024, :], ("QT", 1)), None)
            group(COL["bq"] + 512, 512, 128, 2, "B", (QT_d[1024:1536, :], ("QT", 2)), None)
            group(COL["bkv"] + 512, 256, 128, 4, "B", (KTs_d[l][:, tsl], ("KTs", l, c)), (Vs_d[l][rsl, :], ("Vs", l, c)))
            group(COL["bkv"] + 1024, 256, 128, 5, "B", (KTw_d[l][:, tsl], ("KTw", l, c)), (Vw_d[l][rsl, :], ("Vw", l, c)))
            group(COL["cq"], 512, 128, 6, None, (QT_d[1536:2048, :], ("QT", 3)), None)
            group(COL["ck"], 512, 128, 7, None, (KTc_d[l][:, tsl], ("KTc", l, c)), None)
            group(COL["cv"], 0, 0, None, None, None, (Vc_d[l][rsl, :], ("Vc", l, c)))

            flush()
            si = cnt["stage"] % 2
            cnt["stage"] += 1
            stg, skey = stage[si], ("stage", si)
            slot, wk = wnext([(0, (16, 512), wsrc(Win, 0, 16, COL["bkv"], 512))])
            for t in range(TPC):
                gt = c * TPC + t
                pk, ps = proj_tile(slot, wk, t, 512)
                i = cnt["i"] % 2
                cnt["i"] += 1
                qk_post(ps, pk, 256, 128, None, "B", ob2[i], ("ob", i), gt, qn2[i], ("qn", i), sq2[i], ("sq2", i))
                cp("act", ob2[i][:, 256:512], ps[:, 256:512], [pk], [("ob", i)])
                transposes(ob2[i], ("ob", i), 0, 4, stg, skey, 0, t)
            for b in range(4):
                kv = b // 2
                sv = stg[:, b, :].rearrange("p (n l) -> p n l", l=64)
                tt("dve", sv, sv, posT[:, l, kv, :].unsqueeze(1).to_broadcast([128, CH // 64, 64]), ALU.add,
                   [skey, ("c", "posT")], [skey])
            nb_c = CH // 64
            if DBG["stop"] and l == 0 and c == 0:
                dma("sp", dbgstg_d[:, :], stg[:].rearrange("p b t -> p (b t)"), [skey], [("dbgstg",)])
            for kv in range(2):
                w1 = W["nsa_cmp_w1"][l, kv]
                slot, wk = wnext([(0, (64, 128), w1.rearrange("(k p) c -> p k c", p=128))])
                w1v = wview(slot, 0, 64, 128)
                pk, ps = psum()
                src = stg[:, 2 * kv:2 * kv + 2, :].rearrange("p g (n l) -> p g n l", l=64)
                mm(ps[:, 0:2 * nb_c], [(w1v[:, li, :], src[:, :, :, li]) for li in range(64)], [skey, wk], [pk])
                k1, h1 = small()
                cp("act", h1[:, 0:2 * nb_c], ps[:, 0:2 * nb_c], [pk], [k1])
                if DBG["stop"] and l == 0 and c == 0:
                    dma("sp", dbgh1_d[kv, :, :], h1[:, 0:2 * nb_c], [k1], [("dbgh1", kv)])
                k2, h1g = small()
                gelu_tanh(h1g[:, 0:2 * nb_c], h1[:, 0:2 * nb_c], 2 * nb_c, [k1], [k2])
                if l == 0 and c == 0:
                    dbg_dump(f"h1g{kv}", h1g[:, 0:2 * nb_c], [k2], [128, 2 * nb_c])
                h1b = h1bt
                k3 = ("h1bt",)
                cp("dve", h1b[:, 0:2 * nb_c], h1g[:, 0:2 * nb_c], [k2], [k3])
                dma("pool", w2b[:, kv, :], W["nsa_cmp_w2"][l, kv, :, :], (), [("w2b", kv)])
                pk2, ps2 = psum()
                mm(ps2[0:2 * nb_c, 0:128], [(h1b[:, 0:2 * nb_c], w2b[:, kv, :])], [k3, ("w2b", kv)], [pk2])
                if l == 0 and c == 0:
                    dbg_psum(f"ps2_{kv}", ps2[:, 0:64], pk2, 64)
                if kv == 0:
                    k5, sqs = small()
                    kk, kcn = ("kcn",), kcn_t
                    act(sqc_t[0:2 * nb_c, :], ps2[0:2 * nb_c, 0:128], AF.Square, [pk2], [("sqc",)])
                    rsum(sqs[0:2 * nb_c, 0:1], sqc_t[0:2 * nb_c, :], [("sqc",)], [k5])
                    act(sqs[0:2 * nb_c, 0:1], sqs[0:2 * nb_c, 0:1], AF.Sqrt, [k5, ("c", "eps")], [k5], bias=epsT[0:2 * nb_c, :], scale=1.0 / 128)
                    recip(sqs[0:2 * nb_c, 0:1], sqs[0:2 * nb_c, 0:1], [k5], [k5])
                    stt(kcn[0:2 * nb_c, :], ps2[0:2 * nb_c, 0:128], sqs[0:2 * nb_c, 0:1], hg[0:2 * nb_c, 3, :], ALU.mult, ALU.mult,
                        [pk2, k5, ("c", "hg")], [kk])
                    bk, pb = psumb()
                    tr(pb[:, 0:2 * nb_c], kcn[0:2 * nb_c, :], identb[0:2 * nb_c, 0:2 * nb_c], [kk, ("c", "ident")], [bk])
                    cp("act", kcTs[:, 0:2 * nb_c], pb[:, 0:2 * nb_c], [bk], [("kcTs",)])
                    dma("sp", kcT_d[l][:, :, c * nb_c:(c + 1) * nb_c].rearrange("g d n -> d g n"),
                        kcTs[:, 0:2 * nb_c].rearrange("p (g n) -> p g n", n=nb_c), [("kcTs",)], [("kcT", l, c)])
                else:
                    cp("act", vcs[0:2 * nb_c, :], ps2[0:2 * nb_c, 0:128], [pk2], [("vcs",)])
                    for g in range(2):
                        dma("sp", vcB_d[l][g, c * nb_c:(c + 1) * nb_c, :], vcs[g * nb_c:(g + 1) * nb_c, :], [("vcs",)], [("vcB", l, c)])

            slot, wk = wnext([(0, (16, 24), wsrc(Win, 0, 16, COL["bg"], 24)), (16 * 24, (16, 4), wsrc(Win, 0, 16, COL["cf"], 4))])
            wg_ = wview(slot, 0, 16, 24)
            wf_ = wview(slot, 16 * 24, 16, 4)
            for t in range(TPC):
                gt = c * TPC + t
                pk, ps = psum()
                hreads = [("h", k, t // 4) for k in range(16)] + [wk]
                mm(ps[:, 0:24], [(hT[:, k, t * 128:(t + 1) * 128], wg_[:, k, :]) for k in range(16)], hreads, [pk])
                pkf, psf_ = psum()
                mm(psf_[:, 0:4], [(hT[:, k, t * 128:(t + 1) * 128], wf_[:, k, :]) for k in range(16)], hreads, [pkf])
                act(gates[:, t, :], ps[:, 0:24], AF.Sigmoid, [pk], [("gates", t)])
                k1, z = small()
                tt("dve", z[:, 0:4], psf_[:, 0:4], fbias[:, l, :], ALU.add, [pkf, ("c", "fbias")], [k1])
                act(z[:, 0:4], z[:, 0:4], AF.Exp, [k1], [k1], scale=-1.0)
                act(z[:, 0:4], z[:, 0:4], AF.Ln, [k1], [k1], bias=1.0)
                ts("dve", z[:, 0:4], z[:, 0:4], -1.0, ALU.mult, [k1], [k1])
                pkc, psc = psum()
                mm(psc[:, 0:4], [(trif[:], z[:, 0:4])], [k1, ("c", "trif")], [pkc])
                tt("dve", cum[:, l, gt, :], psc[:, 0:4], carry[:, l, gt, :], ALU.add, [pkc, ("carry", l)], [("cum", l)])
                pkt, pst = psum()
                mm(pst[:, 0:4], [(onesf[:], z[:, 0:4])], [k1, ("c", "onesf")], [pkt])
                tt("dve", carry[:, l, gt + 1, :], pst[:, 0:4], carry[:, l, gt, :], ALU.add, [pkt, ("carry", l)], [("carry", l)])

            for jq in range(12):
                slot, wk = wnext([(0, (16, 512), wsrc(Win, 0, 16, GM0 + jq * 512, 512))])
                wv = wview(slot, 0, 16, 512)
                for j4 in range(4):
                    j = jq * 4 + j4
                    gi = j % 2
                    for th in range(2):
                        tsl2 = slice(th * 512, (th + 1) * 512)
                        pk, ps = psum()
                        mm(ps[:], [(wv[:, k, j4 * 128:(j4 + 1) * 128], hT[:, k, tsl2]) for k in range(16)],
                           [("h", k, th) for k in range(16)] + [wk], [pk])
                        act(gmst[gi][:, tsl2], ps[:], AF.Sigmoid, [pk], [("gmst", gi)])
                    dma("sp", gmT_d[j * 128:(j + 1) * 128, :], gmst[gi], [("gmst", gi)], [("gmT", j)])

        def attention(l, c):
            carv.reset()
            kT2 = [carv.take(SEQ) for _ in range(2)]
            vt2 = [carv.take(NT * 129).rearrange("p (t c) -> p t c", c=129) for _ in range(2)]
            qT2 = [carv.take(CH) for _ in range(2)]
            pT3 = [carv.take(NT * 128).rearrange("p (t q) -> p t q", q=128) for _ in range(2)]
            Msb = carv.take(NT * 128).rearrange("p (t q) -> p t q", q=128)
            qTn = [carv.take(CH) for _ in range(4)]
            ostn = [carv.take(CH) for _ in range(2)]
            osm = [carv.take(128) for _ in range(4)]
            negcum = carv.take(NT * 4, F32).rearrange("p (t h) -> p t h", h=4)
            of2 = [carv.take(128, F32) for _ in range(2)]
            uf2 = [carv.take(128, F32) for _ in range(2)]
            ob2 = [carv.take(128) for _ in range(2)]
            accn = [carv.take(128, F32) for _ in range(4)]
            imp = carv.take(32, F32)
            score = carv.take(32, F32)
            top8 = carv.take(8, F32)
            selb = carv.take(32)
            selTs = carv.take(128)
            pcT = [carv.take(128) for _ in range(2)]
            pcf = [carv.take(128, F32) for _ in range(2)]
            kcT_s = carv.take(64)
            vc_s = carv.take(2 * 129).rearrange("p (g c) -> p g c", c=129)
            ntk = (c + 1) * TPC
            ntok = ntk * 128
            ctr = {"kv": 0, "q": 0, "p": 0, "pw": 0, "o": 0, "os": 0}
            nkeyc = c + 1

            for i in range(2):
                memset("pool", vt2[i][:, :, 128:129], 1.0, [("vt", i)])
            memset("pool", vc_s[:, :, 128:129], 1.0, [("vc_s",)])

            def load_kv(KT, krows, ktag, V, vcols, vtag):
                i = ctr["kv"] % 2
                ctr["kv"] += 1
                dma("sp", kT2[i][:, 0:ntok], KT[krows, 0:ntok], [(ktag, l, cc) for cc in range(nkeyc)], [("kT", i)])
                dma("sp", vt2[i][:, 0:ntk, 0:128], V[0:ntok, vcols].rearrange("(t p) c -> p t c", p=128),
                    [(vtag, l, cc) for cc in range(nkeyc)], [("vt", i)])
                return kT2[i], ("kT", i), vt2[i], ("vt", i)

            def load_q(row0, qtag):
                i = ctr["q"] % 2
                ctr["q"] += 1
                dma("sp", qT2[i][:, :], QT_d[row0:row0 + 128, :], [("QT", qtag)], [("qT", i)])
                return qT2[i], ("qT", i)

            def scores(qT, qkey, kT, kkey, part, t, kts, scale, biasfn=None, maskfn=None, selmask=False, extra_reads=(), pool="p", msb=None, msbkey=None):
                if pool == "w":
                    i = ctr["pw"] % 2
                    ctr["pw"] += 1
                    pT, pkey = pTw[i], ("pTw", i)
                else:
                    i = ctr["p"] % 2
                    ctr["p"] += 1
                    pT, pkey = pT3[i], ("pT", i)
                qs = qT[part, t * 128:(t + 1) * 128]
                for b0 in range(0, len(kts), 4):
                    batch = kts[b0:b0 + 4]
                    pk, ps = psum()

                    def fn(e, batch=batch, ps=ps):
                        ins = None
                        for j, kt in enumerate(batch):
                            ins = e.matmul(ps[:, j * 128:(j + 1) * 128], lhsT=kT[part, kt * 128:(kt + 1) * 128], rhs=qs, start=True, stop=True)
                        return ins
                    S.op("pe", fn, [qkey, kkey], [pk])
                    nb = len(batch)
                    if biasfn is None:
                        act(pT[:, b0:b0 + nb, :], ps[:, 0:nb * 128].rearrange("p (j q) -> p j q", q=128), AF.Exp,
                            [pk], [pkey], scale=scale)
                    else:
                        for j, kt in enumerate(batch):
                            bap, bkey = biasfn(kt)
                            act(pT[:, b0 + j, :], ps[:, j * 128:(j + 1) * 128], AF.Exp, [pk, bkey], [pkey], scale=scale, bias=bap)
                    if selmask:
                        tt("dve", pT[:, b0:b0 + nb, :], pT[:, b0:b0 + nb, :], msb[:, batch[0]:batch[0] + nb, :], ALU.mult,
                           [pkey, msbkey], [pkey])
                    if maskfn is not None:
                        for j, kt in enumerate(batch):
                            m = maskfn(kt)
                            if m is not None:
                                tt("dve", pT[:, b0 + j, :], pT[:, b0 + j, :], m, ALU.mult, [pkey, ("c", "masks")], [pkey])
                return pT, pkey

            def pv(pT, pkey, vt, vkey, kts, ncol=129):
                pk, ps = psum()
                mm(ps[:, 0:ncol], [(pT[:, j, :], vt[:, kt, 0:ncol]) for j, kt in enumerate(kts)], [pkey, vkey], [pk])
                return pk, ps

            def out_T(obf, okey, chunk_idx, t):
                bk, pb = psumb()
                tr(pb[:, 0:128], obf, identb[:], [okey, ("c", "ident")], [bk])
                return bk, pb

            def head_out_begin():
                i = ctr["os"] % 2
                ctr["os"] += 1
                return ostn[i], ("ostn", i)

            def head_out_end(stg, skey, chunk_idx):
                dma("sp", oT_d[chunk_idx * 128:(chunk_idx + 1) * 128, :], stg, [skey], [("oT", chunk_idx)])

            causal = lambda gt: (lambda kt: trib[:] if kt == gt else None)

            sc_a = 64 ** -0.5
            def run_pipeline(units):
                prev = None
                for sfn, ffn in units:
                    ctx = sfn()
                    if prev is not None:
                        prev[1](prev[0])
                    prev = (ctx, ffn)
                if prev is not None:
                    prev[1](prev[0])

            unitsA = []
            stA = {}
            for h in range(4):
                for t in range(TPC):
                    for m in range(2):
                        def sfn(h=h, t=t, m=m):
                            if t == 0 and m == 0:
                                stA["kv"] = load_kv(KTa_d[l], slice(h * 128, (h + 1) * 128), "KTa", Va_d[l], slice(h * 128, (h + 1) * 128), "Va")
                                stA["q"] = load_q(h * 128, 0)
                            kT, kkey, vt, vkey = stA["kv"]
                            qT, qkey = stA["q"]
                            gt = c * TPC + t
                            kts = list(range(gt + 1))
                            part = slice(m * 64, (m + 1) * 64)
                            pT, pkey = scores(qT, qkey, kT, kkey, part, t, kts, sc_a, maskfn=causal(gt))
                            return (pT, pkey, vt, vkey, kts)

                        def ffn_(ctx, h=h, t=t, m=m):
                            pT, pkey, vt, vkey, kts = ctx
                            res = pv(pT, pkey, vt, vkey, kts)
                            if m == 0:
                                if t == 0:
                                    stA["stg"] = head_out_begin()
                                stA["r0"] = res
                                return
                            (pk0, ps0), (pk1, ps1) = stA["r0"], res
                            stg, skey = stA["stg"]
                            k1, r = small()
                            recip(r[:, 0:1], ps0[:, 128:129], [pk0], [k1])
                            recip(r[:, 1:2], ps1[:, 128:129], [pk1], [k1])
                            tt("dve", r[:, 1:2], r[:, 1:2], lamt[:, l, 1:2], ALU.mult, [k1, ("c", "lamt")], [k1])
                            i = ctr["o"] % 2
                            ctr["o"] += 1
                            ts("dve", uf2[i], ps0[:, 0:128], r[:, 0:1], ALU.mult, [pk0, k1], [("uf", i)])
                            stt(of2[i], ps1[:, 0:128], r[:, 1:2], uf2[i], ALU.mult, ALU.add, [pk1, k1, ("uf", i)], [("of", i)])
                            act(uf2[i], of2[i], AF.Square, [("of", i)], [("uf", i)])
                            k2, s2_ = small()
                            rsum(s2_[:, 0:1], uf2[i], [("uf", i)], [k2])
                            act(s2_[:, 0:1], s2_[:, 0:1], AF.Sqrt, [k2, ("c", "eps")], [k2], bias=epsT[:], scale=1.0 / 128)
                            recip(s2_[:, 0:1], s2_[:, 0:1], [k2], [k2])
                            stt(ob2[i], of2[i], s2_[:, 0:1], hg[:, 8, :], ALU.mult, ALU.mult, [("of", i), k2, ("c", "hg")], [("obo", i)])
                            bk, pb = out_T(ob2[i], ("obo", i), h, t)
                            cp("act", stg[:, t * 128:(t + 1) * 128], pb[:, 0:128], [bk], [skey])
                            if t == TPC - 1:
                                head_out_end(stg, skey, h)
                        unitsA.append((sfn, ffn_))
            run_pipeline(unitsA)

            if DBG.get("apart") == "A":
                return
            sc_c = 128 ** -0.5
            ts("dve", negcum[:, 0:ntk, :], cum[:, l, 0:ntk, :], -1.0, ALU.mult, [("cum", l)], [("negcum",)])
            if l == 0 and c == 0:
                dbg_dump("cum", cum[:, l, :, :], [("cum", l)], [128, NT, 4])
                dbg_dump("carry", carry[:, l, :, :], [("carry", l)], [128, NT + 1, 4])
                dbg_dump("gates", gates[:], [("gates", t_) for t_ in range(TPC)], [128, TPC, 24])
                dbg_dump("lamt", lamt[:], [("c", "lamt")], [128, DEPTH, 4])
                dbg_dump("cosT", cosT[:], [("c", "rope")], [128, NT, 24])
            unitsC = []
            stC = {}
            for h in range(4):
                for t in range(TPC):
                    def sfn(h=h, t=t):
                        if t == 0:
                            stC["kv"] = load_kv(KTc_d[l], slice(h * 128, (h + 1) * 128), "KTc", Vc_d[l], slice(h * 128, (h + 1) * 128), "Vc")
                            stC["q"] = load_q(1536 + h * 128, 3)
                        kT, kkey, vt, vkey = stC["kv"]
                        qT, qkey = stC["q"]
                        gt = c * TPC + t
                        kts = list(range(gt + 1))
                        kb_, bt = small()
                        ts("dve", bt[:, 0:gt + 1], negcum[:, 0:gt + 1, h], carry[:, l, gt + 1, h:h + 1], ALU.add,
                           [("negcum",), ("carry", l)], [kb_])
                        pT, pkey = scores(qT, qkey, kT, kkey, slice(0, 128), t, kts, sc_c,
                                          biasfn=lambda kt, bt=bt, kb_=kb_: (bt[:, kt:kt + 1], kb_), maskfn=causal(gt))
                        return (pT, pkey, vt, vkey, kts)

                    def ffn_(ctx, h=h, t=t):
                        pT, pkey, vt, vkey, kts = ctx
                        if t == 0:
                            stC["stg"] = head_out_begin()
                        stg, skey = stC["stg"]
                        pk0, ps0 = pv(pT, pkey, vt, vkey, kts)
                        k1, r = small()
                        recip(r[:, 0:1], ps0[:, 128:129], [pk0], [k1])
                        i = ctr["o"] % 2
                        ctr["o"] += 1
                        ts("dve", ob2[i], ps0[:, 0:128], r[:, 0:1], ALU.mult, [pk0, k1], [("obo", i)])
                        bk, pb = out_T(ob2[i], ("obo", i), 12 + h, t)
                        cp("act", stg[:, t * 128:(t + 1) * 128], pb[:, 0:128], [bk], [skey])
                        if t == TPC - 1:
                            head_out_end(stg, skey, 12 + h)
                    unitsC.append((sfn, ffn_))
            run_pipeline(unitsC)

            if DBG.get("apart") == "C":
                return
            S.barrier()
            carv.reset()
            kT2 = [carv.take(SEQ) for _ in range(2)]
            vt2 = [carv.take(NT * 129).rearrange("p (t c) -> p t c", c=129) for _ in range(2)]
            qTn = [carv.take(CH) for _ in range(4)]
            pT3 = [carv.take(NT * 128).rearrange("p (t q) -> p t q", q=128) for _ in range(2)]
            pTw = [carv.take(5 * 128).rearrange("p (t q) -> p t q", q=128) for _ in range(2)]
            Msb2 = [carv.take(NT * 128).rearrange("p (t q) -> p t q", q=128) for _ in range(2)]
            ob2 = [carv.take(128) for _ in range(2)]
            accn2 = [[carv.take(128, F32) for _ in range(4)] for _ in range(2)]
            imp2 = [carv.take(32, F32) for _ in range(2)]
            score2 = [carv.take(32, F32) for _ in range(2)]
            top82 = [carv.take(8, F32) for _ in range(2)]
            selb2 = [carv.take(32) for _ in range(2)]
            selTs2 = [carv.take(128) for _ in range(2)]
            pcT = [carv.take(128) for _ in range(4)]
            pcf = [carv.take(128, F32) for _ in range(4)]
            kcT_s = carv.take(64)
            vc_s = carv.take(2 * 129).rearrange("p (g c) -> p g c", c=129)
            osm = [carv.take(128) for _ in range(4)]
            for i in range(2):
                memset("pool", vt2[i][:, :, 128:129], 1.0, [("vt", i)])
            memset("pool", vc_s[:, :, 128:129], 1.0, [("vc_s",)])
            sc_b = 128 ** -0.5
            nblk = (c + 1) * (CH // 64)
            dma("sp", kcT_s[:, 0:2 * nblk].rearrange("p (g n) -> p g n", n=nblk), kcT_d[l][:, :, 0:nblk].rearrange("g d n -> d g n"),
                [("kcT", l, cc) for cc in range(nkeyc)], [("kcT_s",)])
            dma("sp", vc_s[0:nblk, :, 0:128], vcB_d[l][:, 0:nblk, :].rearrange("g n d -> n g d"),
                [("vcB", l, cc) for cc in range(nkeyc)], [("vc_s",)])
            for g in range(2):
                kTs, kskey, vts, vskey = load_kv(KTs_d[l], slice(g * 128, (g + 1) * 128), "KTs", Vs_d[l], slice(g * 128, (g + 1) * 128), "Vs")
                kTw, kwkey, vtw, vwkey = load_kv(KTw_d[l], slice(g * 128, (g + 1) * 128), "KTw", Vw_d[l], slice(g * 128, (g + 1) * 128), "Vw")
                qh = []
                for hh in range(4):
                    qh.append((qTn[hh], ("qTn", hh)))
                    dma("sp", qTn[hh][:, :], QT_d[512 + (g * 4 + hh) * 128:512 + (g * 4 + hh + 1) * 128, :],
                        [("QT", 1 + g)], [("qTn", hh)])
                def stage_ab(t, g=g, qh=qh):
                    gt = c * TPC + t
                    par = t % 2
                    Msb, accn, imp, score, top8, selb, selTs = Msb2[par], accn2[par], imp2[par], score2[par], top82[par], selb2[par], selTs2[par]
                    KM, KA, KI, KS, KT8, KSB, KST = ("Msb", par), ("accn", par), ("imp", par), ("score", par), ("top8", par), ("selb", par), ("selTs", par)
                    for hh in range(4):
                        qT, qkey = qh[hh]
                        pk, ps = psum()
                        mm(ps[0:nblk, 0:128], [(kcT_s[:, g * nblk:(g + 1) * nblk], qT[:, t * 128:(t + 1) * 128])], [("kcT_s",), qkey], [pk])
                        act(pcf[hh][0:nblk, :], ps[0:nblk, 0:128], AF.Exp, [pk], [("pcf", hh)], scale=sc_b)
                        tt("dve", pcT[hh][0:nblk, :], pcf[hh][0:nblk, :], cmaskb[0:nblk, gt * 128:(gt + 1) * 128], ALU.mult,
                           [("pcf", hh), ("c", "masks")], [("pcT", hh)])
                    for hh in range(4):
                        h = g * 4 + hh
                        pko, pso = psum()
                        mm(pso[:, 0:129], [(pcT[hh][0:nblk, :], vc_s[0:nblk, g, :])], [("pcT", hh), ("vc_s",)], [pko])
                        pkp, psp = psum()
                        mm(psp[:, 0:nblk], [(pcT[hh][0:nblk, :], identb[0:nblk, 0:nblk])], [("pcT", hh), ("c", "ident")], [pkp])
                        k1, r = small()
                        ts("dve", r[:, 0:1], pso[:, 128:129], 1e-30, ALU.add, [pko], [k1])
                        recip(r[:, 0:1], r[:, 0:1], [k1], [k1])
                        tt("dve", r[:, 1:2], r[:, 0:1], gates[:, t, h * 3:h * 3 + 1], ALU.mult, [k1, ("gates", t)], [k1])
                        ts("dve", accn[hh], pso[:, 0:128], r[:, 1:2], ALU.mult, [pko, k1], [(KA, hh)])
                        if hh == 0:
                            if nblk < 32:
                                memset("dve", imp[:, nblk:32], 0.0, [KI])
                            ts("dve", imp[:, 0:nblk], psp[:, 0:nblk], r[:, 0:1], ALU.mult, [pkp, k1], [KI])
                        else:
                            stt(imp[:, 0:nblk], psp[:, 0:nblk], r[:, 0:1], imp[:, 0:nblk], ALU.mult, ALU.add, [pkp, k1, KI], [KI])
                    tt("dve", score, imp, selA[:, gt, :], ALU.mult, [KI, ("c", "selA")], [KS])
                    tt("dve", score, score, selB[:, gt, :], ALU.add, [KS, ("c", "selB")], [KS])
                    S.op("dve", lambda e, top8=top8, score=score: e.max(out=top8, in_=score), [KS], [KT8])
                    ts("dve", selb, score, top8[:, 7:8], ALU.is_ge, [KS, KT8], [KSB])
                    bk, pb = psumb()
                    tr(pb[0:32, 0:128], selb, identb[:], [KSB, ("c", "ident")], [bk])
                    cp("act", selTs[0:32, :], pb[0:32, 0:128], [bk], [KST])
                    kts = list(range(gt + 1))
                    for b0_ in range(0, len(kts), 4):
                        batch = kts[b0_:b0_ + 4]
                        pk, ps = psum()

                        def fn(e, batch=batch, ps=ps, selTs=selTs):
                            ins = None
                            for j, kt in enumerate(batch):
                                ins = e.matmul(ps[:, j * 128:(j + 1) * 128], lhsT=Eb[0:32, kt * 128:(kt + 1) * 128], rhs=selTs[0:32, :], start=True, stop=True)
                            return ins
                        S.op("pe", fn, [KST, ("c", "masks")], [pk])
                        nb = len(batch)
                        cp("act", Msb[:, b0_:b0_ + nb, :], ps[:, 0:nb * 128].rearrange("p (j q) -> p j q", q=128), [pk], [KM])
                    tt("dve", Msb[:, gt, :], Msb[:, gt, :], trib[:], ALU.mult, [KM, ("c", "masks")], [KM])

                def stage_c(t, g=g, qh=qh):
                    gt = c * TPC + t
                    par = t % 2
                    Msb, accn = Msb2[par], accn2[par]
                    KM, KA = ("Msb", par), ("accn", par)
                    kts = list(range(gt + 1))
                    wk0 = max(0, gt - 4)
                    wkts = list(range(wk0, gt + 1))

                    def wmask(kt, gt=gt):
                        if kt == gt:
                            return trib[:]
                        if kt == gt - 4:
                            return bandb[:]
                        return None
                    unitsB = []
                    stB = {}
                    for hh in range(4):
                        def s_sel(hh=hh, t=t, kts=kts):
                            qT, qkey = qh[hh]
                            return scores(qT, qkey, kTs, kskey, slice(0, 128), t, kts, sc_b, selmask=True, msb=Msb, msbkey=KM)

                        def f_sel(ctx, hh=hh, kts=kts):
                            stB[hh] = pv(ctx[0], ctx[1], vts, vskey, kts)

                        def s_win(hh=hh, t=t, wkts=wkts, wmask=wmask):
                            qT, qkey = qh[hh]
                            return scores(qT, qkey, kTw, kwkey, slice(0, 128), t, wkts, sc_b, maskfn=wmask, pool="w")

                        def f_win(ctx, hh=hh, t=t, wkts=wkts, g=g):
                            h = g * 4 + hh
                            pkw, psw = pv(ctx[0], ctx[1], vtw, vwkey, wkts)
                            pks, pss = stB[hh]
                            k1, r = small()
                            recip(r[:, 0:1], pss[:, 128:129], [pks], [k1])
                            recip(r[:, 1:2], psw[:, 128:129], [pkw], [k1])
                            tt("dve", r[:, 0:2], r[:, 0:2], gates[:, t, h * 3 + 1:h * 3 + 3], ALU.mult, [k1, ("gates", t)], [k1])
                            stt(accn[hh], pss[:, 0:128], r[:, 0:1], accn[hh], ALU.mult, ALU.add, [pks, k1, (KA, hh)], [(KA, hh)])
                            i = ctr["o"] % 2
                            ctr["o"] += 1
                            stt(ob2[i], psw[:, 0:128], r[:, 1:2], accn[hh], ALU.mult, ALU.add, [pkw, k1, (KA, hh)], [("obo", i)])
                            bk, pb = out_T(ob2[i], ("obo", i), 4 + h, t)
                            oi = ctr["os"] % 4
                            ctr["os"] += 1
                            cp("act", osm[oi], pb[:, 0:128], [bk], [("osm", oi)])
                            dma("sp", oT_d[(4 + h) * 128:(5 + h) * 128, t * 128:(t + 1) * 128], osm[oi], [("osm", oi)], [("oT", 4 + h)])
                        unitsB += [(s_sel, f_sel), (s_win, f_win)]
                    run_pipeline(unitsB)

                stage_ab(0)
                for t in range(TPC):
                    if t + 1 < TPC:
                        stage_ab(t + 1)
                    stage_c(t)

        def merge(l):
            carv.reset()
            oT = carv.take(16 * CH).rearrange("p (k t) -> p k t", t=CH)
            gm2 = [carv.take(3 * CH).rearrange("p (i t) -> p i t", t=CH) for _ in range(2)]
            t1 = [carv.take(512, F32) for _ in range(1)]
            t2 = [carv.take(512, F32) for _ in range(1)]
            t3 = [carv.take(512, F32) for _ in range(1)]
            for k in range(16):
                dma("sp", oT[:, k, :], oT_d[k * 128:(k + 1) * 128, :], [("oT", k)], [("oTs", k)])
            Wa, Wb, Wc, Wo = W["w_branch_a"][l], W["w_branch_b"][l], W["w_branch_c"][l], W["w_out"][l]
            n = 0
            for cq in range(4):
                slot, wk = wnext([(0, (4, 512), wsrc(Wa, 0, 4, cq * 512, 512)),
                                  (2048, (8, 512), wsrc(Wb, 0, 8, cq * 512, 512)),
                                  (6144, (4, 512), wsrc(Wc, 0, 4, cq * 512, 512))])
                wa = wview(slot, 0, 4, 512)
                wb = wview(slot, 2048, 8, 512)
                wc = wview(slot, 6144, 4, 512)
                for c4 in range(4):
                    c = cq * 4 + c4
                    gi = c % 2
                    dma("sp", gm2[gi], gmT_d.rearrange("(i r) t -> r i t", i=3)[c * 128:(c + 1) * 128, :, :],
                        [("gmT", i * 16 + c) for i in range(3)], [("gm", gi)])
                    csl = slice(c4 * 128, (c4 + 1) * 128)
                    for th in range(2):
                        tsl = slice(th * 512, (th + 1) * 512)
                        i = 0
                        pka, psa = psum()
                        mm(psa[:], [(wa[:, k, csl], oT[:, k, tsl]) for k in range(4)], [("oTs", k) for k in range(4)] + [wk], [pka])
                        pkb, psb_ = psum()
                        mm(psb_[:], [(wb[:, k, csl], oT[:, 4 + k, tsl]) for k in range(8)], [("oTs", 4 + k) for k in range(8)] + [wk], [pkb])
                        pkc, psc = psum()
                        mm(psc[:], [(wc[:, k, csl], oT[:, 12 + k, tsl]) for k in range(4)], [("oTs", 12 + k) for k in range(4)] + [wk], [pkc])
                        tt("dve", t1[i], psa[:], gm2[gi][:, 0, tsl], ALU.mult, [pka, ("gm", gi)], [("t1", i)])
                        tt("dve", t2[i], psb_[:], gm2[gi][:, 1, tsl], ALU.mult, [pkb, ("gm", gi)], [("t2", i)])
                        tt("dve", t3[i], psc[:], gm2[gi][:, 2, tsl], ALU.mult, [pkc, ("gm", gi)], [("t3", i)])
                        tt("pool", t1[i], t1[i], t2[i], ALU.add, [("t1", i), ("t2", i)], [("t1", i)])
                        tt("pool", hT[:, c, tsl], t1[i], t3[i], ALU.add, [("t1", i), ("t3", i)], [("h", c, th)])
            for cq in range(4):
                slot, wk = wnext([(0, (16, 512), wsrc(Wo, 0, 16, cq * 512, 512))])
                wo = wview(slot, 0, 16, 512)
                for c4 in range(4):
                    c = cq * 4 + c4
                    for th in range(2):
                        tsl = slice(th * 512, (th + 1) * 512)
                        pk, ps = psum()
                        mm(ps[:], [(wo[:, k, c4 * 128:(c4 + 1) * 128], hT[:, k, tsl]) for k in range(16)],
                           [("h", k, th) for k in range(16)] + [wk], [pk])
                        tt("dve", xT[:, c, tsl], ps[:], xT[:, c, tsl], ALU.add, [pk, ("x", c, th)], [("x", c, th)])


        def program():
            outkeys = []

            def dump_x(c):
                for k in range(16):
                    for th in range(2):
                        dma("sp", outT_d[k * 128:(k + 1) * 128, c * CH + th * 512:c * CH + (th + 1) * 512], xT[:, k, th * 512:(th + 1) * 512],
                            [("x", k, th)], [("out", c, k, th)])
                        outkeys.append(("out", c, k, th))

            def check(name, l, c):
                if DBG["stop"] == (name, l, c):
                    dump_x(c)
                    dma("sp", dbgh_d[:, :], hT[:].rearrange("p k t -> p (k t)"), [("h", k, th) for k in range(16) for th in range(2)], [("dbgh",)])
                    outkeys.append(("dbgh",))
                    raise _Stop()

            try:
                setup()
                S.barrier()
                for c in range(NCHUNK):
                    for k in range(16):
                        for th in range(2):
                            dma("sp", xT[:, k, th * 512:(th + 1) * 512], xT_d[k * 128:(k + 1) * 128, c * CH + th * 512:c * CH + (th + 1) * 512],
                                (), [("x", k, th)])
                    for l in range(DEPTH):
                        load_layer(l)
                        rmsnorm(l, 0)
                        ffn(l, "ffn1")
                        rmsnorm(l, 1)
                        projection(l, c)
                        S.barrier()
                        attention(l, c)
                        S.barrier()
                        merge(l)
                        S.barrier()
                        rmsnorm(l, 2)
                        ffn(l, "ffn2")
                    dump_x(c)
            except _Stop:
                pass
            S.barrier()
            S.op("sp", None, outkeys + DBG.get("keys", []), ())

        S.collect = True
        program()
        S.collect = False
        prot["f"] = prot["b"] = 0
        rot["smallf"] = 0
        program()
        S.emit(nc, st)
    return nc, consts


_CACHE = {}


def kernel(**inputs):
    if "nc" not in _CACHE:
        _CACHE["nc"] = build_program()
    nc, consts = _CACHE["nc"]
    x = np.asarray(inputs["x"], dtype=np.float32)
    pos = np.asarray(inputs["positions"], dtype=np.int32)
    B = x.shape[0]
    shared = {}
    for k, v in inputs.items():
        if k in ("x", "positions"):
            continue
        a = np.ascontiguousarray(np.asarray(v, dtype=np.float32))
        if k == "da_lambda":
            a = a.reshape(DEPTH, 256)
        if k == "nsa_k_norm":
            a = a.reshape(DEPTH, 384)
        if k in ("ffn1_norm", "mix_norm", "ffn2_norm"):
            a = np.ascontiguousarray(a.reshape(DEPTH, 16, 128).transpose(0, 2, 1))
        if k == "nsa_cmp_pos":
            a = np.ascontiguousarray(a.transpose(0, 1, 3, 2))
        shared[k] = a
    for k, v in consts.items():
        shared["c_" + k] = v
    zero_shared = {k: np.zeros_like(v) for k, v in shared.items()}
    zx = np.zeros((D, SEQ), np.float32)
    zp = np.zeros((128, NT), np.int32)
    in_maps = []
    slot_of = {}
    bi = 0
    for core in range(8):
        if core in ACTIVE_CORES:
            m = dict(shared)
            m["xT"] = np.ascontiguousarray(x[bi].T)
            m["pos"] = np.ascontiguousarray(pos[bi].reshape(NT, 128).T)
            slot_of[bi] = core
            bi += 1
        else:
            m = dict(zero_shared)
            m["xT"] = zx
            m["pos"] = zp
        in_maps.append(m)
    res = run_bass_kernel_spmd(nc, in_maps, core_ids=list(range(8)))
    out = np.stack([np.ascontiguousarray(res.results[slot_of[b]]["outT"].T) for b in range(B)], axis=0)
    return out.astype(np.float32)
```
